# Optimizing a Trainium2 kernel written in Bass

```python
import jax, jax.numpy as jnp
from jax import lax
import numpy as np

D_MODEL = 1024
BATCH = 32
SEQ = 256
DEPTH = 2
DEC_BATCH = 4
DEC_SEQ = 2048
PAST_LEN = 256

GRID_W = 64
HEAD_DIM = 128
N_Q_HEADS = 4
N_KV_HEADS = 2
Q_PER_KV = N_Q_HEADS // N_KV_HEADS
ATTN_WIDTH = N_Q_HEADS * HEAD_DIM
KV_WIDTH = N_KV_HEADS * HEAD_DIM
WINDOW = 128
BLOCK = 128
ATTN_SCALE = HEAD_DIM ** -0.5
ROPE_BASE = 10000.0
NEG_INF = -1e30
RNN_WIDTH = 512
RNN_BLOCKS = 8
RNN_BLOCK_W = RNN_WIDTH // RNN_BLOCKS
CONV_W = 4
CONV_PAD_LEFT = 2
LRU_C = 8.0
AB_IN_WIDTH = ATTN_WIDTH + 2 * KV_WIDTH + 2 * RNN_WIDTH
AB_SPLITS = (ATTN_WIDTH, ATTN_WIDTH + KV_WIDTH, ATTN_WIDTH + 2 * KV_WIDTH, ATTN_WIDTH + 2 * KV_WIDTH + RNN_WIDTH)
AB_MIX_WIDTH = ATTN_WIDTH + RNN_WIDTH
CHUNK = 128
SGU_WIDTH = D_MODEL
SGU_GROUPS = 8
SGU_GROUP_W = SGU_WIDTH // SGU_GROUPS
N_EXPERTS = 16
EXPERT_FF = 2048
EC_FACTOR = 2
N_EVEN = (DEPTH + 1) // 2
N_ODD = DEPTH // 2
ALPHA = (2 * DEPTH) ** 0.25
BETA = (8 * DEPTH) ** -0.25
LN_EPS = 1e-6

kernel_name = "hybrid_diffusion_trunk_step"


def layer_norm(x, g, b):
    xf = x.astype(jnp.float32)
    mu = xf.mean(-1, keepdims=True)
    var = jnp.square(xf - mu).mean(-1, keepdims=True)
    return ((xf - mu) * lax.rsqrt(var + LN_EPS)).astype(x.dtype) * g + b


def adaln(cvec, w, b):
    m = jax.nn.silu(cvec) @ w + b
    return jnp.split(m[:, None, :], 6, axis=-1)


def modulate(x, shift, scale):
    return x * (1 + scale) + shift


def axial_rope(x):
    T = x.shape[1]
    rows = T // GRID_W
    row = jnp.repeat(jnp.arange(rows, dtype=jnp.float32), GRID_W)
    col = jnp.tile(jnp.arange(GRID_W, dtype=jnp.float32), rows)
    n_freq = HEAD_DIM // 4
    freqs = ROPE_BASE ** (-jnp.arange(n_freq, dtype=jnp.float32) / n_freq)
    ang = jnp.concatenate([row[:, None] * freqs, col[:, None] * freqs], axis=-1)
    cos = jnp.cos(ang)[None, :, None, :]
    sin = jnp.sin(ang)[None, :, None, :]
    xf = x.astype(jnp.float32)
    x1, x2 = xf[..., 0::2], xf[..., 1::2]
    out = jnp.stack([x1 * cos - x2 * sin, x1 * sin + x2 * cos], axis=-1).reshape(x.shape)
    return out.astype(x.dtype)


def sink_softmax(scores, sink):
    s = jnp.concatenate([scores, jnp.broadcast_to(sink, scores.shape[:-1] + (1,))], axis=-1)
    return jax.nn.softmax(s, axis=-1)[..., :-1]


def ab_project(h, in_w):
    B, T, _ = h.shape
    q, k, v, xr, xg = jnp.split(h @ in_w, AB_SPLITS, axis=-1)
    return (q.reshape(B, T, N_Q_HEADS, HEAD_DIM), k.reshape(B, T, N_KV_HEADS, HEAD_DIM),
            v.reshape(B, T, N_KV_HEADS, HEAD_DIM), xr, xg)


def context_attention(q, k, v, sink):
    B, S = q.shape[:2]
    qg = q.reshape(B, S, N_KV_HEADS, Q_PER_KV, HEAD_DIM)
    s = jnp.einsum('bqkgd,bskd->bkgqs', qg, k).astype(jnp.float32) * ATTN_SCALE
    p = sink_softmax(s, sink.reshape(N_KV_HEADS, Q_PER_KV)[:, :, None, None].astype(jnp.float32))
    o = jnp.einsum('bkgqs,bskd->bqkgd', p.astype(v.dtype), v)
    return o.reshape(B, S, ATTN_WIDTH)


def latent_attention(q, k, v, k_ctx, v_ctx, sink):
    B, T = q.shape[:2]
    P = k_ctx.shape[1]
    nb = T // BLOCK
    qb = q.reshape(B, nb, BLOCK, N_KV_HEADS, Q_PER_KV, HEAD_DIM)

    def bands(t):
        tp = jnp.pad(t, ((0, 0), (BLOCK, BLOCK), (0, 0), (0, 0))).reshape(B, nb + 2, BLOCK, N_KV_HEADS, HEAD_DIM)
        return jnp.concatenate([tp[:, :-2], tp[:, 1:-1], tp[:, 2:]], axis=2)

    kw, vw = bands(k), bands(v)
    blk = jnp.arange(nb)[:, None, None] * BLOCK
    qpos = blk + jnp.arange(BLOCK)[None, :, None]
    kpos = blk - BLOCK + jnp.arange(3 * BLOCK)[None, None, :]
    valid = (jnp.abs(qpos - kpos) <= WINDOW) & (kpos >= 0) & (kpos < T)
    s_win = jnp.einsum('bnqkgd,bnskd->bnkgqs', qb, kw).astype(jnp.float32) * ATTN_SCALE
    s_win = jnp.where(valid[None, :, None, None], s_win, NEG_INF)
    s_ctx = jnp.einsum('bnqkgd,bskd->bnkgqs', qb, k_ctx).astype(jnp.float32) * ATTN_SCALE
    p = sink_softmax(jnp.concatenate([s_ctx, s_win], axis=-1),
                     sink.reshape(N_KV_HEADS, Q_PER_KV)[:, :, None, None].astype(jnp.float32)).astype(v.dtype)
    o = (jnp.einsum('bnkgqs,bskd->bnqkgd', p[..., :P], v_ctx)
         + jnp.einsum('bnkgqs,bnskd->bnqkgd', p[..., P:], vw))
    return o.reshape(B, T, ATTN_WIDTH)


def centred_conv(x, w, b):
    T = x.shape[1]
    xp = jnp.pad(x, ((0, 0), (CONV_PAD_LEFT, CONV_W - 1 - CONV_PAD_LEFT), (0, 0)))
    return sum(xp[:, i:i + T] * w[i] for i in range(CONV_W)) + b


def block_diag(x, w, b):
    B, T, _ = x.shape
    y = jnp.einsum('btnc,ncd->btnd', x.reshape(B, T, RNN_BLOCKS, RNN_BLOCK_W), w)
    return y.reshape(B, T, RNN_WIDTH) + b


def _linrec(e1, e2):
    a1, b1 = e1
    a2, b2 = e2
    return a1 * a2, a2 * b1 + b2


def lru_scan(x, wa, ba, wx, bx, lam, h0):
    r = jax.nn.sigmoid(block_diag(x, wa, ba).astype(jnp.float32))
    i = jax.nn.sigmoid(block_diag(x, wx, bx).astype(jnp.float32))
    log_a = -LRU_C * r * jax.nn.softplus(-lam.astype(jnp.float32))
    a = jnp.exp(log_a)
    u = jnp.sqrt(-jnp.expm1(2.0 * log_a)) * (i * x.astype(jnp.float32))
    u = u.at[:, 0].add(a[:, 0] * h0.astype(jnp.float32))
    _, h = lax.associative_scan(_linrec, (a, u), axis=1)
    return h


def rglru_bidirectional(xr, xg, conv_w, conv_b, wa, ba, wx, bx, lam, h0):
    xc = centred_conv(xr, conv_w, conv_b)
    hf = lru_scan(xc, wa[0], ba[0], wx[0], bx[0], lam[0], h0[:, 0])
    hb = jnp.flip(lru_scan(jnp.flip(xc, axis=1), wa[1], ba[1], wx[1], bx[1], lam[1], h0[:, 1]), axis=1)
    y = (hf + hb).astype(xr.dtype) * jax.nn.gelu(xg)
    return y, hf, hb


def sgu_mixer(h, in_w, in_b, ln_g, ln_b, sp_w, sp_b, out_w):
    B, T, _ = h.shape
    nc = T // CHUNK
    u, v = jnp.split(jax.nn.gelu(h @ in_w + in_b), 2, axis=-1)
    v = layer_norm(v, ln_g, ln_b).reshape(B, nc, CHUNK, SGU_GROUPS, SGU_GROUP_W)
    mixed = jnp.einsum('gpq,bnqgc->bnpgc', sp_w, v) + sp_b.T[None, None, :, :, None]
    return (u * mixed.reshape(B, T, SGU_WIDTH)) @ out_w


def expert_choice_moe(x, router_w, w1, w3, w2):
    B, T, D = x.shape
    cap = EC_FACTOR * T // N_EXPERTS
    aff = jax.nn.softmax((x @ router_w).astype(jnp.float32), axis=-1)
    gate, idx = lax.top_k(jnp.swapaxes(aff, 1, 2), cap)
    xg = jax.vmap(lambda xb, ib: xb[ib])(x, idx)
    hid = jax.nn.silu(jnp.einsum('becd,edf->becf', xg, w1)) * jnp.einsum('becd,edf->becf', xg, w3)
    yg = jnp.einsum('becf,efd->becd', hid, w2) * gate[..., None].astype(x.dtype)
    return jax.vmap(lambda ib, yb: jnp.zeros((T, D), yb.dtype).at[ib.reshape(-1)].add(yb.reshape(-1, D)))(idx, yg)


def setup_inputs(seed: int = 0) -> dict:
    key = jax.random.key(seed)
    ks = iter(jax.random.split(key, 64))
    f32 = jnp.float32

    def nrm(shape, std):
        return jax.random.normal(next(ks), shape, f32) * std

    def gain(shape):
        return 1.0 + nrm(shape, 0.02)

    a_init = jax.random.uniform(next(ks), (N_EVEN, 2, RNN_WIDTH), f32, minval=0.9, maxval=0.999) ** (1.0 / LRU_C)
    return {
        "x_prompt": nrm((BATCH, SEQ, D_MODEL), 1.0),
        "x_sample": nrm((DEC_BATCH, DEC_SEQ, D_MODEL), 1.0),
        "cache_k": nrm((DEC_BATCH, N_EVEN, PAST_LEN, N_KV_HEADS, HEAD_DIM), 1.0),
        "cache_v": nrm((DEC_BATCH, N_EVEN, PAST_LEN, N_KV_HEADS, HEAD_DIM), 1.0),
        "state_rglru": nrm((DEC_BATCH, N_EVEN, 2, RNN_WIDTH), 0.5),
        "c": nrm((DEC_BATCH, D_MODEL), 1.0),
        "c_ctx": nrm((D_MODEL,), 1.0),
        "mod_w": nrm((DEPTH, D_MODEL, 6 * D_MODEL), 0.5 * D_MODEL ** -0.5),
        "mod_b": nrm((DEPTH, 6 * D_MODEL), 0.02),
        "ln_mix_g": gain((DEPTH, D_MODEL)),
        "ln_mix_b": nrm((DEPTH, D_MODEL), 0.02),
        "ln_ffn_g": gain((DEPTH, D_MODEL)),
        "ln_ffn_b": nrm((DEPTH, D_MODEL), 0.02),
        "ab_in_w": nrm((N_EVEN, D_MODEL, AB_IN_WIDTH), D_MODEL ** -0.5),
        "attn_sink": nrm((N_EVEN, N_Q_HEADS), 0.5),
        "rnn_conv_w": nrm((N_EVEN, CONV_W, RNN_WIDTH), CONV_W ** -0.5),
        "rnn_conv_b": nrm((N_EVEN, RNN_WIDTH), 0.02),
        "lru_wa": nrm((N_EVEN, 2, RNN_BLOCKS, RNN_BLOCK_W, RNN_BLOCK_W), RNN_BLOCK_W ** -0.5),
        "lru_ba": nrm((N_EVEN, 2, RNN_WIDTH), 0.05),
        "lru_wx": nrm((N_EVEN, 2, RNN_BLOCKS, RNN_BLOCK_W, RNN_BLOCK_W), RNN_BLOCK_W ** -0.5),
        "lru_bx": nrm((N_EVEN, 2, RNN_WIDTH), 0.05),
        "lru_lambda": jnp.log(a_init) - jnp.log1p(-a_init),
        "ab_out_w": nrm((N_EVEN, AB_MIX_WIDTH, D_MODEL), BETA * AB_MIX_WIDTH ** -0.5),
        "sgu_in_w": nrm((N_ODD, D_MODEL, 2 * SGU_WIDTH), D_MODEL ** -0.5),
        "sgu_in_b": nrm((N_ODD, 2 * SGU_WIDTH), 0.02),
        "sgu_ln_g": gain((N_ODD, SGU_WIDTH)),
        "sgu_ln_b": nrm((N_ODD, SGU_WIDTH), 0.02),
        "sgu_spatial_w": nrm((N_ODD, SGU_GROUPS, CHUNK, CHUNK), 0.5 * CHUNK ** -0.5),
        "sgu_spatial_b": 1.0 + nrm((N_ODD, SGU_GROUPS, CHUNK), 0.1),
        "sgu_out_w": nrm((N_ODD, SGU_WIDTH, D_MODEL), BETA * SGU_WIDTH ** -0.5),
        "router_w": nrm((DEPTH, D_MODEL, N_EXPERTS), D_MODEL ** -0.5),
        "moe_w1": nrm((DEPTH, N_EXPERTS, D_MODEL, EXPERT_FF), D_MODEL ** -0.5),
        "moe_w3": nrm((DEPTH, N_EXPERTS, D_MODEL, EXPERT_FF), D_MODEL ** -0.5),
        "moe_w2": nrm((DEPTH, N_EXPERTS, EXPERT_FF, D_MODEL), BETA * EXPERT_FF ** -0.5),
    }


def reference(x_prompt, x_sample, cache_k, cache_v, state_rglru, c, c_ctx,
              mod_w, mod_b, ln_mix_g, ln_mix_b, ln_ffn_g, ln_ffn_b,
              ab_in_w, attn_sink, rnn_conv_w, rnn_conv_b, lru_wa, lru_ba, lru_wx, lru_bx, lru_lambda, ab_out_w,
              sgu_in_w, sgu_in_b, sgu_ln_g, sgu_ln_b, sgu_spatial_w, sgu_spatial_b, sgu_out_w,
              router_w, moe_w1, moe_w3, moe_w2):
    xp, xs = x_prompt, x_sample
    ctx_keys, ctx_vals, ctx_states = [], [], []
    for l in range(DEPTH):
        mp = adaln(c_ctx[None], mod_w[l], mod_b[l])
        ms = adaln(c, mod_w[l], mod_b[l])
        hp = modulate(xp, mp[0], mp[1])
        hs = modulate(xs, ms[0], ms[1])
        e = l // 2
        if l % 2 == 0:
            q, k, v, xr, xg = ab_project(hp, ab_in_w[e])
            att = context_attention(q, k, v, attn_sink[e])
            h0 = jnp.zeros((xp.shape[0], 2, RNN_WIDTH), xp.dtype)
            rnn, hf, hb = rglru_bidirectional(xr, xg, rnn_conv_w[e], rnn_conv_b[e], lru_wa[e], lru_ba[e],
                                             lru_wx[e], lru_bx[e], lru_lambda[e], h0)
            op = jnp.concatenate([att, rnn], axis=-1) @ ab_out_w[e]
            ctx_keys.append(k)
            ctx_vals.append(v)
            ctx_states.append(jnp.stack([hf[:, -1], hb[:, 0]], axis=1).astype(xp.dtype))
            q, k, v, xr, xg = ab_project(hs, ab_in_w[e])
            att = latent_attention(axial_rope(q), axial_rope(k), v, cache_k[:, e], cache_v[:, e], attn_sink[e])
            rnn, _, _ = rglru_bidirectional(xr, xg, rnn_conv_w[e], rnn_conv_b[e], lru_wa[e], lru_ba[e],
                                           lru_wx[e], lru_bx[e], lru_lambda[e], state_rglru[:, e])
            os_ = jnp.concatenate([att, rnn], axis=-1) @ ab_out_w[e]
        else:
            op = sgu_mixer(hp, sgu_in_w[e], sgu_in_b[e], sgu_ln_g[e], sgu_ln_b[e],
                           sgu_spatial_w[e], sgu_spatial_b[e], sgu_out_w[e])
            os_ = sgu_mixer(hs, sgu_in_w[e], sgu_in_b[e], sgu_ln_g[e], sgu_ln_b[e],
                            sgu_spatial_w[e], sgu_spatial_b[e], sgu_out_w[e])
        xp = layer_norm(ALPHA * xp + mp[2] * op, ln_mix_g[l], ln_mix_b[l])
        xs = layer_norm(ALPHA * xs + ms[2] * os_, ln_mix_g[l], ln_mix_b[l])
        fp = expert_choice_moe(modulate(xp, mp[3], mp[4]), router_w[l], moe_w1[l], moe_w3[l], moe_w2[l])
        fs = expert_choice_moe(modulate(xs, ms[3], ms[4]), router_w[l], moe_w1[l], moe_w3[l], moe_w2[l])
        xp = layer_norm(ALPHA * xp + mp[5] * fp, ln_ffn_g[l], ln_ffn_b[l])
        xs = layer_norm(ALPHA * xs + ms[5] * fs, ln_ffn_g[l], ln_ffn_b[l])
    new_cache_k = jnp.stack(ctx_keys, axis=1)
    new_cache_v = jnp.stack(ctx_vals, axis=1)
    new_state_rglru = jnp.stack(ctx_states, axis=1)
    return (xp, xs, new_cache_k, new_cache_v, new_state_rglru)
```

```python
import os
import numpy as np
from contextlib import ExitStack
import concourse.bass as bass
import concourse.mybir as mybir
from concourse.bass_utils import run_bass_kernel_spmd

F32 = mybir.dt.float32
BF16 = mybir.dt.bfloat16
U32 = mybir.dt.uint32
AF = mybir.ActivationFunctionType
ALU = mybir.AluOpType

P = 128
T = 2048
D = 1024
NT = 16
KC = 8
NEGM = -30000.0
ALPHA = 4.0 ** 0.25
LN_EPS = 1e-6
ATTN_SCALE = 128.0 ** -0.5
NE = 16
ROPE_ENG = 'dve' if os.environ.get('MK_DBG') == 'e' else 'pool'
GRP = 2

VR = {}


def _reg(name, n):
    VR[name] = (len_vr[0], n)
    len_vr[0] += n


len_vr = [0]
_reg('c', 8)
for _l in range(2):
    _reg(f'modb{_l}_0', 8)
    _reg(f'modb{_l}_1', 8)
_reg('convw', 16)
_reg('convb', 4)
_reg('ba', 8)
_reg('bx', 8)
_reg('lam', 8)
_reg('h0', 8)
_reg('sgub_u', 8)
_reg('flags', 4)
_reg('sink', 4)
NV = len_vr[0]
assert NV <= 128

RR = {}
_r = 0
for _l in range(2):
    for _i in range(2, 6):
        RR[f'modb{_l}_{_i}'] = _r
        _r += 1
    for _n in ('ln_mix_g', 'ln_mix_b', 'ln_ffn_g', 'ln_ffn_b'):
        RR[f'{_n}{_l}'] = _r
        _r += 1
for _n in ('sgub_v', 'sgu_ln_g', 'sgu_ln_b', 'spb'):
    RR[_n] = _r
    _r += 1
NR = _r


class Sched:
    ENG = ['pe', 'act', 'dve', 'pool', 'sp']

    def __init__(self):
        self.prog = {e: [] for e in self.ENG}
        self.cnt = {}
        self.seen = {e: {} for e in self.ENG}
        self.lastw = {}
        self.readers = {}
        self.nops = 0
        self.dmaidx = {}

    def _need(self, eng, dep):
        sem, val = dep[0], dep[1]
        if self.seen[eng].get(sem, 0) >= val:
            return
        self.seen[eng][sem] = val
        self.prog[eng].append(('wait', sem, val))

    def op(self, eng, emit, reads=(), writes=(), dma=False, semkey=None):
        if dma:
            if semkey is None:
                j = self.dmaidx.get(eng, 0)
                self.dmaidx[eng] = j + 1
                semkey = f'{eng}{j % 8}'
            sem = 'd_' + semkey
            if self.cnt.get(sem, 0) > 0:
                self._need(eng, (sem, self.cnt[sem]))
        else:
            sem = 's_' + eng
        bx = set()
        for k in list(reads) + list(writes):
            kb = k[0] if isinstance(k, tuple) else k
            if isinstance(kb, str) and kb[:2] == 'ps' and kb[2:].isdigit():
                bx.add(('bankx', kb))
        if bx:
            writes = list(writes) + sorted(bx)
        for k in reads:
            w = self.lastw.get(k)
            if w:
                self._need(eng, w)
        for k in writes:
            w = self.lastw.get(k)
            if w and (w[2] != eng or dma or w[0][0] == 'd'):
                self._need(eng, w)
            for r in self.readers.get(k, ()):
                if r[2] != eng or eng != 'pe' or dma or r[0][0] == 'd':
                    self._need(eng, r)
        inc = 16 if dma else 1
        self.cnt[sem] = self.cnt.get(sem, 0) + inc
        me = (sem, self.cnt[sem], eng)
        self.prog[eng].append(('op', emit, sem, inc))
        for k in reads:
            self.readers.setdefault(k, []).append(me)
        for k in writes:
            self.lastw[k] = me
            self.readers[k] = []
        self.nops += 1

    def barrier(self):
        for e in self.ENG:
            for sem, v in self.cnt.items():
                self._need(e, (sem, v))
        self.lastw = {}
        self.readers = {}

    def replay(self, nc, es):
        sems = {sem: es.enter_context(nc.semaphore(sem)) for sem in self.cnt}
        block = es.enter_context(nc.Block())

        def run(name):
            def f(e):
                for it in self.prog[name]:
                    if it[0] == 'wait':
                        e.wait_ge(sems[it[1]], it[2])
                    else:
                        it[1](e).then_inc(sems[it[2]], it[3])
            return f
        block.tensor(run('pe'))
        block.scalar(run('act'))
        block.vector(run('dve'))
        block.gpsimd(run('pool'))
        block.sync(run('sp'))


class Arena:
    def __init__(self, tensor, base, size):
        self.t, self.base, self.size, self.top = tensor, base, size, 0

    def alloc(self, shape, dt):
        bpe = 2 if dt == BF16 else 4
        n = int(np.prod(shape[1:]))
        off = (self.top + 63) // 64 * 64
        nb = n * bpe
        assert off + nb <= self.size, f"arena overflow {off + nb} > {self.size}"
        self.top = off + nb
        b = (self.base + off) // 2
        ap = self.t[:, b:b + nb // 2]
        if dt != BF16:
            ap = ap.bitcast(dt)
        if len(shape) == 3:
            ap = ap.rearrange("p (a b) -> p a b", b=shape[2])
        elif len(shape) == 4:
            ap = ap.rearrange("p (a b c) -> p a b c", b=shape[2], c=shape[3])
        if shape[0] != P:
            ap = ap[0:shape[0]]
        return ap

    def mark(self):
        return self.top

    def release(self, m):
        self.top = m

    def sub(self, size):
        off = (self.top + 63) // 64 * 64
        assert off + size <= self.size
        self.top = off + size
        return Arena(self.t, self.base + off, size)


class WStream:
    def __init__(self, S, slots, plan=None):
        self.S, self.slots, self.plan = S, slots, plan
        self.rec = []
        self.i = 0
        self.issued = 0
        self.free = [0, 1, 2, 3]
        self.slot_of = {}
        self.wide = False
        self.prev = None

    def set_wide(self, wide):
        self.wide = wide
        if wide:
            for s in (4, 5):
                if s not in self.free:
                    self.free.append(s)
            self._issue()
        else:
            self.free = [s for s in self.free if s < 4]

    def _issue(self):
        if self.plan is None:
            return
        while self.issued < len(self.plan) and self.free:
            tag, src, shape, moe = self.plan[self.issued]
            s = None
            for cand in self.free:
                if moe or cand < 4:
                    s = cand
                    break
            if s is None:
                break
            self.free.remove(s)
            dst = self._view(s, shape)
            self.S.op('pool', lambda e, dst=dst, src=src: e.dma_start(out=dst, in_=src),
                      reads=(), writes=[f'slot{s}'], dma=True, semkey=f'slot{s}')
            self.slot_of[self.issued] = s
            self.issued += 1

    def _view(self, s, shape):
        n = int(np.prod(shape[1:]))
        ap = self.slots[s][:, 0:n]
        if len(shape) == 3:
            ap = ap.rearrange("p (a b) -> p a b", b=shape[2])
        return ap

    def next(self, tag, src, shape, moe=False):
        if self.plan is None:
            self.rec.append((tag, src, shape, moe))
            return self._view(len(self.rec) % 4, shape), f'slot{len(self.rec) % 4}'
        assert self.plan[self.i][0] == tag, (self.plan[self.i][0], tag)
        self._issue()
        assert self.i in self.slot_of, f"weight stream starved (no slot) at {tag}"
        s = self.slot_of.pop(self.i)
        self.i += 1
        return self._view(s, shape), f'slot{s}'

    def release(self, key):
        if self.plan is None:
            return
        s = int(key[4:])
        if s < 4 or self.wide:
            self.free.append(s)
        self._issue()


def build_program(stage=99, cut=99):
    nc = bass.Bass("TRN2", target_bir_lowering=False)

    def din(name, shape):
        return nc.dram_tensor(name, list(shape), F32, kind="ExternalInput").ap()

    x_in = din("x_in", [T, D])
    vecs = din("vecs", [NV, 128])
    rowsrc = din("rowsrc", [NR, D])
    kctx_d = din("kctxT", [128, 512])
    vctx_d = din("vctx", [128, 512])
    mask_d = din("maskLR", [128, 16 * 2 * 128])
    rope_d = din("rope", [128, 2 * T])
    mod_w = din("mod_w", [2, D, 6 * D])
    w_in = din("w_in", [D, 2816])
    ab_out_w = din("ab_out_w", [D, D])
    bd_d = din("bd", [16, 128, 128])
    sgu_in_w = din("sgu_in_w", [D, 2 * D])
    spwT_d = din("spwT", [128, 8 * 128])
    sgu_out_w = din("sgu_out_w", [D, D])
    router_w = din("router_w", [2, D, NE])
    if stage >= 2:
        moe_w1 = din("moe_w1", [2, NE, D, 2 * D])
        moe_w3 = din("moe_w3", [2, NE, D, 2 * D])
        moe_w2 = din("moe_w2", [2, NE, 2 * D, D])
    y_d = nc.dram_tensor("y", [T, D], F32, kind="ExternalOutput").ap()
    kv_d = nc.dram_tensor("kv_out", [T, 512], F32, kind="ExternalOutput").ap()
    st_d = nc.dram_tensor("st_out", [8, 1024], F32, kind="ExternalOutput").ap()

    es = ExitStack()
    with es:
        ARENA_BYTES = 211968
        arena_t = es.enter_context(nc.sbuf_tensor("arena", [P, ARENA_BYTES // 2], BF16))
        banks = [es.enter_context(nc.psum_tensor(f"ps{i}", [P, 512], F32)) for i in range(8)]
        A = Arena(arena_t, 0, ARENA_BYTES)
        identf = A.alloc([P, 128], F32)
        identb = A.alloc([P, 128], BF16)
        onesb = A.alloc([P, 128], BF16)
        onesf = A.alloc([P, 128], F32)
        iota128 = A.alloc([P, 128], F32)
        tcol = A.alloc([P, 16], F32)
        Vc = A.alloc([P, NV], F32)
        sc = A.alloc([P, 8], BF16)
        scB = A.alloc([P, 8, 128], BF16)
        modc = A.alloc([P, 2, 16], F32)
        esink = A.alloc([P, 4], F32)
        nsp = A.alloc([P, 8], F32)
        rw = A.alloc([P, 2, 8, 16], BF16)
        small = A.alloc([P, 64], F32)
        xres = A.alloc([P, NT, D], F32)
        P1_base = A.top - NT * D * 4
        p2 = A.alloc([P, 16384], BF16)
        hT = p2.rearrange("p (a b) -> p a b", b=T)
        xm_tok = p2.rearrange("p (a b) -> p a b", b=D)
        slots = [A.alloc([P, 4096], BF16) for _ in range(4)]
        slot4_base = (A.top + 63) // 64 * 64
        slots += [A.alloc([P, 4096], BF16) for _ in range(2)]
        assert A.top == slot4_base + 16384
        P4 = A.sub(ARENA_BYTES - ((A.top + 63) // 64 * 64))
        R1 = Arena(arena_t, (P1_base + 63) // 64 * 64, NT * D * 4 - 64)

        def vcol(name, j=0, n=1):
            b = VR[name][0] + j
            return Vc[:, b:b + n]

        def emit_all(S, ws):
            def mm(out, lhsT, rhs, start, stop, reads, writes):
                S.op('pe', lambda e: e.matmul(out, lhsT, rhs, start=start, stop=stop), reads, writes)

            def tr(out, in_, ident, reads, writes):
                S.op('pe', lambda e: e.transpose(out, in_, ident), reads, writes)

            def act(out, in_, func, reads, writes, bias=None, scale=None):
                kw = {}
                if bias is not None:
                    kw['bias'] = bias
                if scale is not None:
                    kw['scale'] = scale
                S.op('act', lambda e: e.activation(out=out, in_=in_, func=func, **kw), reads, writes)

            def tt(out, in0, in1, op, reads, writes, eng='dve'):
                S.op(eng, lambda e: e.tensor_tensor(out=out, in0=in0, in1=in1, op=op), reads, writes)

            def ts(out, in0, s1, s2, op0, op1, reads, writes, eng='dve'):
                if s2 is None:
                    S.op(eng, lambda e: e.tensor_scalar(out=out, in0=in0, scalar1=s1, scalar2=None, op0=op0), reads, writes)
                else:
                    S.op(eng, lambda e: e.tensor_scalar(out=out, in0=in0, scalar1=s1, scalar2=s2, op0=op0, op1=op1), reads, writes)

            def stt(out, in0, scalar, in1, op0, op1, reads, writes, eng='dve'):
                S.op(eng, lambda e: e.scalar_tensor_tensor(out=out, in0=in0, scalar=scalar, in1=in1, op0=op0, op1=op1), reads, writes)

            def cp(out, in_, reads, writes, eng='dve'):
                S.op(eng, lambda e: e.tensor_copy(out=out, in_=in_), reads, writes)

            def dma(out, in_, reads, writes, q='sp'):
                S.op(q, lambda e: e.dma_start(out=out, in_=in_), reads, writes, dma=True)

            def bank_bf(i):
                return banks[i][:, :].bitcast(BF16)

            def wpiece(tag, W2d, c0, ncols, moe=False, rows=None):
                src = W2d.rearrange("(kc p) n -> p kc n", p=128)[:, :, c0:c0 + ncols]
                kc = W2d.shape[0] // 128
                return ws.next(tag, src, [P, kc, ncols], moe)

            S.op('pool', lambda e: e.iota(iota128, [[1, 128]], base=0, channel_multiplier=0,
                                          allow_small_or_imprecise_dtypes=True), (), ['iota128'])
            S.op('pool', lambda e: e.iota(identf, [[1, 128]], base=0, channel_multiplier=-1,
                                          allow_small_or_imprecise_dtypes=True), (), ['identf'])
            S.op('pool', lambda e: e.iota(tcol, [[128, 16]], base=0, channel_multiplier=1,
                                          allow_small_or_imprecise_dtypes=True), (), ['tcol'])
            S.op('pool', lambda e: e.memset(onesb, 1.0), (), ['onesb'])
            S.op('pool', lambda e: e.memset(onesf, 1.0), (), ['onesf'])
            S.op('pool', lambda e: e.memset(small[:, 0:1], -0.5), (), ['mhalf'])
            ts(identf, identf, 0.0, None, ALU.is_equal, None, ['identf'], ['identf'])
            cp(identb, identf, ['identf'], ['identb'])
            m0 = R1.mark()
            vst = R1.alloc([NV, 128], F32)
            dma(vst, vecs[:, :], (), ['vst'])
            tr(banks[0][:, 0:NV], vst, identf[0:NV, 0:NV], ['vst', 'identf'], ['ps0'])
            cp(Vc, banks[0][:, 0:NV], ['ps0'], ['Vc'])
            for l in range(2):
                dma(rw[:, l], router_w[l].rearrange("(kc p) n -> p kc n", p=128), (), ['rw'], q='pool')
            act(sc, vcol('c', 0, 8), AF.Silu, ['Vc'], ['sc'])
            cp(scB, sc.unsqueeze(2).to_broadcast([P, 8, 128]), ['sc'], ['scB'])
            act(esink, vcol('sink', 0, 4), AF.Exp, ['Vc'], ['esink'])
            act(nsp, vcol('lam', 0, 8), AF.Exp, ['Vc'], ['nsp'], scale=-1.0)
            act(nsp, nsp, AF.Ln, ['nsp'], ['nsp'], bias=1.0)
            ts(nsp, nsp, -8.0, None, ALU.mult, None, ['nsp'], ['nsp'])
            carry = vcol('flags', 0)
            ctxb = vcol('flags', 1)
            fS = vcol('flags', 2)
            fP = vcol('flags', 3)
            S.barrier()
            R1.release(m0)

            def adaln_cols(l):
                for i in range(2):
                    for half in range(2):
                        wv, wk = wpiece(f'modc{l}_{i}_{half}', mod_w[l], i * 1024 + half * 512, 512)
                        for oc in range(4):
                            col = i * 8 + half * 4 + oc
                            for kc in range(KC):
                                mm(banks[0][:, col:col + 1], wv[:, kc, oc * 128:(oc + 1) * 128], sc[:, kc:kc + 1],
                                   kc == 0, kc == KC - 1, [wk, 'sc'], ['ps0'])
                        ws.release(wk)
                tt(modc[:, l, 0:8], banks[0][:, 0:8], vcol(f'modb{l}_0', 0, 8), ALU.add, ['ps0', 'Vc'], ['modc'])
                tt(modc[:, l, 8:16], banks[0][:, 8:16], vcol(f'modb{l}_1', 0, 8), ALU.add, ['ps0', 'Vc'], ['modc'])
                ts(modc[:, l, 8:16], modc[:, l, 8:16], 1.0, None, ALU.add, None, ['modc'], ['modc'])

            def adaln_row(l, i, dst, key, plus1=False):
                dma(dst, rowsrc[RR[f'modb{l}_{i}']:RR[f'modb{l}_{i}'] + 1, :].partition_broadcast(P), (), [key])
                for half in range(2):
                    wv, wk = wpiece(f'modr{l}_{i}_{half}', mod_w[l], i * 1024 + half * 512, 512)
                    b = 1 + half
                    for kc in range(KC):
                        mm(banks[b][:, :], scB[:, kc, :], wv[:, kc, :], kc == 0, kc == KC - 1, [wk, 'scB'], [f'ps{b}'])
                    ws.release(wk)
                    tt(dst[:, half * 512:(half + 1) * 512], dst[:, half * 512:(half + 1) * 512], banks[b][:, :], ALU.add,
                       [f'ps{b}', key], [key])
                if plus1:
                    ts(dst, dst, 1.0, None, ALU.add, None, [key], [key])

            def load_row(dst, name, key):
                dma(dst, rowsrc[RR[name]:RR[name] + 1, :].partition_broadcast(P), (), [key])

            def make_hT(l, src_chunk, g4s=(0, 1, 2, 3)):
                for g4 in g4s:
                    for kc in range(KC):
                        b = (g4 * KC + kc) % 4
                        for j in range(4):
                            tc = g4 * 4 + j
                            xa, xk = src_chunk(tc)
                            tr(banks[b][:, j * 128:(j + 1) * 128], xa[:, kc * 128:(kc + 1) * 128], identf,
                               [xk, 'identf'], [f'ps{b}'])
                        act(hT[:, kc, g4 * 512:(g4 + 1) * 512], banks[b][:, :], AF.Identity, [f'ps{b}', 'modc'],
                            [('p2', kc, g4)], bias=modc[:, l, kc:kc + 1], scale=modc[:, l, 8 + kc:9 + kc])

            hT_keys = [('p2', kc, g) for kc in range(KC) for g in range(4)]

            def mixer_epilogue(l, mixT, mix_keys, Wd, tagp, rows):
                wv = []
                for dh in range(2):
                    wv.append(wpiece(f'{tagp}{dh}', Wd, dh * 512, 512))
                m = P4.mark()
                tmpg = [P4.alloc([P, 512], F32) for _ in range(2)]
                xnb_ = [P4.alloc([P, D], F32) for _ in range(2)]
                stt_ = P4.alloc([P, 32], F32)
                for tc in range(NT):
                    xn = xnb_[tc % 2]
                    xnk = f'xn{tc % 2}'
                    act(xres[:, tc, :], xres[:, tc, :], AF.Copy, [('xres', tc)], [('xres', tc)], scale=ALPHA)
                    for dh in range(2):
                        b = 4 + (tc * 2 + dh) % 4
                        for kc in range(KC):
                            mm(banks[b][:, :], mixT[kc][:, tc * 128:(tc + 1) * 128], wv[dh][0][:, kc, :], kc == 0, kc == KC - 1,
                               [wv[dh][1]] + mix_keys, [f'ps{b}'])
                        tg_ = tmpg[dh]
                        tt(tg_, banks[b][:, :], rows['g2'][:, dh * 512:(dh + 1) * 512], ALU.mult, [f'ps{b}', 'row_g2'], [f'tmpg{dh}'])
                        xs = xres[:, tc, dh * 512:(dh + 1) * 512]
                        tt(xs, xs, tg_, ALU.add, [('xres', tc), f'tmpg{dh}'], [('xres', tc)])
                    layer_norm_chunk(tc, stt_, xn, rows['lng'], rows['lnb'], 'row_lng', 'row_lnb')
                    tt(xn, xres[:, tc, :], rows['s4p'], ALU.mult, [('xres', tc), 'row_s4p'], [xnk], eng='pool')
                    tt(xm_tok[:, tc, :], xn, rows['s3'], ALU.add, [xnk, 'row_s3'], [('xm', tc)], eng='pool')
                ws.release(wv[0][1])
                ws.release(wv[1][1])
                P4.release(m)

            def layer_norm_chunk(tc, st2, xn, lng, lnb, kg, kb, tail_eng='dve'):
                xc_ = xres[:, tc, :]
                i2 = tc % 2
                st_ = st2[:, 16 * i2:16 * i2 + 16]
                xnk = f'xn{i2}'
                S.op('dve', lambda e: e.bn_stats(out=st_[:, 0:6], in_=xres[:, tc, 0:512]), [('xres', tc)], [f'lnst{i2}'])
                S.op('dve', lambda e: e.bn_stats(out=st_[:, 6:12], in_=xres[:, tc, 512:1024]), [('xres', tc)], [f'lnst{i2}'])
                S.op('dve', lambda e: e.bn_aggr(out=st_[:, 12:14], in_=st_[:, 0:12]), [f'lnst{i2}'], [f'lnmv{i2}'])
                ts(st_[:, 14:15], st_[:, 13:14], LN_EPS, None, ALU.add, None, [f'lnmv{i2}'], [f'lnr{i2}'], eng='pool')
                tt(st_[:, 14:15], st_[:, 14:15], small[:, 0:1], ALU.pow, [f'lnr{i2}', 'mhalf'], [f'lnr{i2}'], eng='pool')
                ts(st_[:, 15:16], st_[:, 12:13], st_[:, 14:15], -1.0, ALU.mult, ALU.mult, [f'lnmv{i2}', f'lnr{i2}'], [f'lnn{i2}'])
                act(xn, xc_, AF.Identity, [('xres', tc), f'lnr{i2}', f'lnn{i2}'], [xnk], bias=st_[:, 15:16], scale=st_[:, 14:15])
                tt(xn, xn, lng, ALU.mult, [xnk, kg], [xnk])
                tt(xc_, xn, lnb, ALU.add, [xnk, kb], [('xres', tc)], eng=tail_eng)

            def layer0_mixer():
                adaln_cols(0)
                m4 = P4.mark()
                attT = P4.alloc([P, 4, T], BF16)
                rnnT = P4.alloc([P, 4, T], BF16)
                rows = {}
                for nm in ('g2', 'lng', 'lnb', 's4p'):
                    rows[nm] = B2.alloc([P, D], F32)
                rows['s3'] = P4.alloc([P, D], F32)
                if cut == 1:
                    return
                m1 = R1.mark()
                xst = [R1.alloc([P, 4, D], F32) for _ in range(2)]
                for g4 in range(4):
                    dma(xst[g4 % 2], x_in[g4 * 512:(g4 + 1) * 512, :].rearrange("(j p) d -> p j d", p=128), (), [f'xst{g4 % 2}'])
                    for kc in range(KC):
                        b = (g4 * KC + kc) % 4
                        for j in range(4):
                            tr(banks[b][:, j * 128:(j + 1) * 128], xst[g4 % 2][:, j, kc * 128:(kc + 1) * 128], identf,
                               [f'xst{g4 % 2}', 'identf'], [f'ps{b}'])
                        if os.environ.get("MK_DBG") == "a":
                            continue
                        if os.environ.get("MK_DBG") == "c":
                            act(hT[:, kc, g4 * 512:(g4 + 1) * 512], banks[b][:, :], AF.Identity, [f'ps{b}', 'modc'], [('p2', kc, g4)],
                                bias=modc[:, 0, kc:kc + 1])
                            continue
                        if os.environ.get("MK_DBG") == "d":
                            act(hT[:, kc, g4 * 512:(g4 + 1) * 512], banks[b][:, :], AF.Identity, [f'ps{b}', 'modc'], [('p2', kc, g4)],
                                scale=modc[:, 0, 8 + kc:9 + kc])
                            continue
                        if os.environ.get("MK_DBG") == "b":
                            act(hT[:, kc, g4 * 512:(g4 + 1) * 512], banks[b][:, :], AF.Copy, [f'ps{b}', 'modc'], [('p2', kc, g4)])
                            continue
                        act(hT[:, kc, g4 * 512:(g4 + 1) * 512], banks[b][:, :], AF.Identity, [f'ps{b}', 'modc'],
                            [('p2', kc, g4)], bias=modc[:, 0, kc:kc + 1], scale=modc[:, 0, 8 + kc:9 + kc])
                S.barrier()
                R1.release(m1)
                adaln_row(0, 2, rows['g2'], 'row_g2')
                adaln_row(0, 3, rows['s3'], 'row_s3')
                adaln_row(0, 4, rows['s4p'], 'row_s4p', plus1=True)
                load_row(rows['lng'], 'ln_mix_g0', 'row_lng')
                load_row(rows['lnb'], 'ln_mix_b0', 'row_lnb')
                if cut == 2:
                    return
                qT = R1.alloc([P, 4, T], BF16)
                kT = R1.alloc([P, 2, T], BF16)
                vtok = R1.alloc([P, NT, 256], BF16)
                m2 = R1.mark()
                ropeC = R1.alloc([P, T], F32)
                ropeS = R1.alloc([P, T], F32)
                t1 = [R1.alloc([P, 512], F32) for _ in range(2)]
                t2 = [R1.alloc([P, 512], F32) for _ in range(2)]
                kvst = [R1.alloc([P, 512], F32) for _ in range(2)]
                dma(ropeC, rope_d[:, 0:T], (), ['ropeC'])
                dma(ropeS, rope_d[:, T:2 * T], (), ['ropeS'])
                wq, wqk = wpiece('w_q', w_in, 0, 512)
                wqs, wqsk = wpiece('w_qsw', w_in, 2048, 512)
                it = 0
                for h in range(4):
                    for tg in range(4):
                        for kc in range(KC):
                            mm(banks[0][:, :], wq[:, kc, h * 128:(h + 1) * 128], hT[:, kc, tg * 512:(tg + 1) * 512],
                               kc == 0, kc == KC - 1, [wqk] + hT_keys, ['ps0'])
                        for kc in range(KC):
                            mm(banks[1][:, :], wqs[:, kc, h * 128:(h + 1) * 128], hT[:, kc, tg * 512:(tg + 1) * 512],
                               kc == 0, kc == KC - 1, [wqsk] + hT_keys, ['ps1'])
                        i2 = it % 2
                        it += 1
                        tt(t1[i2], banks[0][:, :], ropeC[:, tg * 512:(tg + 1) * 512], ALU.mult, ['ps0', 'ropeC'], [f't1{i2}'])
                        tt(t2[i2], banks[1][:, :], ropeS[:, tg * 512:(tg + 1) * 512], ALU.mult, ['ps1', 'ropeS'], [f't2{i2}'])
                        tt(qT[:, h, tg * 512:(tg + 1) * 512], t1[i2], t2[i2], ALU.add, [f't1{i2}', f't2{i2}'], ['qT'], eng=ROPE_ENG)
                ws.release(wqk)
                ws.release(wqsk)
                if cut == 21:
                    return
                wkv, wkvk = wpiece('w_kv', w_in, 512, 512)
                wks, wksk = wpiece('w_ksw', w_in, 2560, 256)
                for h in range(2):
                    for tg in range(4):
                        for kc in range(KC):
                            mm(banks[0][:, :], wkv[:, kc, h * 128:(h + 1) * 128], hT[:, kc, tg * 512:(tg + 1) * 512],
                               kc == 0, kc == KC - 1, [wkvk] + hT_keys, ['ps0'])
                        for kc in range(KC):
                            mm(banks[1][:, :], wks[:, kc, h * 128:(h + 1) * 128], hT[:, kc, tg * 512:(tg + 1) * 512],
                               kc == 0, kc == KC - 1, [wksk] + hT_keys, ['ps1'])
                        i2 = it % 2
                        it += 1
                        tt(t1[i2], banks[0][:, :], ropeC[:, tg * 512:(tg + 1) * 512], ALU.mult, ['ps0', 'ropeC'], [f't1{i2}'])
                        tt(t2[i2], banks[1][:, :], ropeS[:, tg * 512:(tg + 1) * 512], ALU.mult, ['ps1', 'ropeS'], [f't2{i2}'])
                        tt(kT[:, h, tg * 512:(tg + 1) * 512], t1[i2], t2[i2], ALU.add, [f't1{i2}', f't2{i2}'], ['kT'], eng=ROPE_ENG)
                ws.release(wksk)
                if cut == 22:
                    return
                for tc in range(NT):
                    b = 2 + tc % 2
                    for kc in range(KC):
                        mm(banks[b][:, :], hT[:, kc, tc * 128:(tc + 1) * 128], wkv[:, kc, :], kc == 0, kc == KC - 1,
                           [wkvk] + hT_keys, [f'ps{b}'])
                    i2 = tc % 2
                    act(kvst[i2], banks[b][:, :], AF.Copy, [f'ps{b}'], [f'kvst{i2}'])
                    cp(vtok[:, tc, :], banks[b][:, 256:512], [f'ps{b}'], ['vtok'])
                    if os.environ.get("MK_DBG2") != "g":
                        dma(kv_d[tc * 128:(tc + 1) * 128, :], kvst[i2], [f'kvst{i2}'], ['kv_d'])
                ws.release(wkvk)
                S.barrier()
                R1.release(m2)
                if cut == 3:
                    return
                maskLR = R1.alloc([P, 16, 2, 128], BF16)
                kctx = R1.alloc([P, 2, 256], BF16)
                vctx = R1.alloc([P, 2, 256], BF16)
                sm = [R1.alloc([P, 512], F32) for _ in range(2)]
                pT = [R1.alloc([P, 5, 256], BF16) for _ in range(2)]
                rden = [R1.alloc([P, 256], F32) for _ in range(2)]
                dma(maskLR, mask_d[:, :].rearrange("p (n s q) -> p n s q", s=2, q=128), (), ['maskLR'], q='pool')
                dma(kctx, kctx_d[:, :].rearrange("p (k t) -> p k t", t=256), (), ['kctx'], q='pool')
                dma(vctx, vctx_d[:, :].rearrange("p (b f) -> p b f", f=256), (), ['vctx'], q='pool')
                def att_A(itn):
                    n, kv = itn // 2, itn % 2
                    nl = max(n - 1, 0)
                    nr = min(n + 1, NT - 1)
                    i2 = itn % 2
                    bs0, bs1, bs2, bo = (0, 1, 2, 3) if i2 == 0 else (4, 5, 6, 7)
                    qa = qT[:, 2 * kv:2 * kv + 2, n * 128:(n + 1) * 128]
                    keyb = [kctx[:, kv, 0:128], kctx[:, kv, 128:256], kT[:, kv, nl * 128:(nl + 1) * 128],
                            kT[:, kv, n * 128:(n + 1) * 128], kT[:, kv, nr * 128:(nr + 1) * 128]]
                    outs = [banks[bs0][:, 0:256], banks[bs0][:, 256:512], banks[bs1][:, 0:256], banks[bs1][:, 256:512],
                            banks[bs2][:, 0:256]]
                    okeys = [f'ps{bs0}', f'ps{bs0}', f'ps{bs1}', f'ps{bs1}', f'ps{bs2}']
                    for kb in range(5):
                        mm(outs[kb], keyb[kb], qa, True, True, ['qT', 'kT', 'kctx'], [okeys[kb]])
                    p_ = pT[i2]
                    pk = f'pT{i2}'
                    act(p_[:, 0:2, :], banks[bs0][:, :].rearrange("p (a b) -> p a b", b=256), AF.Exp, [f'ps{bs0}', 'Vc'], [pk],
                        bias=ctxb, scale=ATTN_SCALE)
                    s_ = sm[i2]
                    mL = maskLR[:, n, 0, :].unsqueeze(1).to_broadcast([P, 2, 128])
                    mR = maskLR[:, n, 1, :].unsqueeze(1).to_broadcast([P, 2, 128])
                    stt(s_[:, 0:256].rearrange("p (a b) -> p a b", b=128), banks[bs1][:, 0:256].rearrange("p (a b) -> p a b", b=128),
                        ATTN_SCALE, mL, ALU.mult, ALU.add, [f'ps{bs1}', 'maskLR'], [f'sm{i2}'])
                    stt(s_[:, 256:512].rearrange("p (a b) -> p a b", b=128), banks[bs2][:, 0:256].rearrange("p (a b) -> p a b", b=128),
                        ATTN_SCALE, mR, ALU.mult, ALU.add, [f'ps{bs2}', 'maskLR'], [f'sm{i2}'])
                    act(p_[:, 3, :], banks[bs1][:, 256:512], AF.Exp, [f'ps{bs1}'], [pk], scale=ATTN_SCALE)
                    act(p_[:, 2, :], s_[:, 0:256], AF.Exp, [f'sm{i2}'], [pk])
                    act(p_[:, 4, :], s_[:, 256:512], AF.Exp, [f'sm{i2}'], [pk])

                def att_B(itn):
                    n, kv = itn // 2, itn % 2
                    nl = max(n - 1, 0)
                    nr = min(n + 1, NT - 1)
                    i2 = itn % 2
                    bo = 3 if i2 == 0 else 7
                    p_ = pT[i2]
                    pk = f'pT{i2}'
                    vb = [vctx[:, 0, kv * 128:(kv + 1) * 128], vctx[:, 1, kv * 128:(kv + 1) * 128],
                          vtok[:, nl, kv * 128:(kv + 1) * 128], vtok[:, n, kv * 128:(kv + 1) * 128],
                          vtok[:, nr, kv * 128:(kv + 1) * 128]]
                    for kb in range(5):
                        mm(banks[bo][:, 0:256], vb[kb], p_[:, kb, :], kb == 0, kb == 4, [pk, 'vtok', 'vctx'], [f'ps{bo}'])
                    for kb in range(5):
                        mm(banks[bo][:, 256:512], onesb, p_[:, kb, :], kb == 0, kb == 4, [pk, 'onesb'], [f'ps{bo}'])
                    rd = rden[i2]
                    for g in range(2):
                        act(rd[:, g * 128:(g + 1) * 128], banks[bo][:, 256 + g * 128:256 + (g + 1) * 128], AF.Ln,
                            [f'ps{bo}', 'esink'], [f'rden{i2}'], bias=esink[:, 2 * kv + g:2 * kv + g + 1])
                    act(rd, rd, AF.Exp, [f'rden{i2}'], [f'rden{i2}'], scale=-1.0)
                    tt(attT[:, 2 * kv:2 * kv + 2, n * 128:(n + 1) * 128], banks[bo][:, 0:256].rearrange("p (a b) -> p a b", b=128),
                       rd.rearrange("p (a b) -> p a b", b=128), ALU.mult, [f'ps{bo}', f'rden{i2}'], ['attT'])

                NIT = NT * 2
                att_A(0)
                for itn in range(NIT):
                    if itn + 1 < NIT:
                        att_A(itn + 1)
                    att_B(itn)
                S.barrier()
                R1.release(0)
                if cut == 4:
                    return
                mrg = P4.mark()
                bdm = P4.alloc([P, 16, 128], BF16)
                dma(bdm, bd_d.rearrange("m p q -> p m q"), (), ['bdm'], q='pool')
                xrp = R1.alloc([P, 8, 259], F32)
                xc = R1.alloc([P, 8, 256], F32)
                xcb = R1.alloc([P, T], BF16)
                a_t = R1.alloc([P, T], F32)
                i_t = R1.alloc([P, T], F32)
                a_t2 = P4.alloc([P, T], F32)
                i_t2 = P4.alloc([P, T], F32)
                tmp = R1.alloc([P, T], F32)
                hf = R1.alloc([P, T], F32)
                hb = R1.alloc([P, T], F32)
                stT = P4.alloc([P, 64], F32)
                stO = P4.alloc([64, 128], F32)
                wxr, wxrk = wpiece('w_xr', w_in, 1024, 512)
                wxg, wxgk = wpiece('w_xg', w_in, 1536, 512)
                xc2 = xc.rearrange("p a b -> p (a b)")
                def rg_front(c):
                    S.op('pool', lambda e: e.memset(xrp, 0.0), (), ['xrp'])
                    for tg in range(4):
                        b = tg % 2
                        for kc in range(KC):
                            mm(banks[b][:, :], wxr[:, kc, c * 128:(c + 1) * 128], hT[:, kc, tg * 512:(tg + 1) * 512],
                               kc == 0, kc == KC - 1, [wxrk] + hT_keys, [f'ps{b}'])
                        act(xrp[:, 2 * tg:2 * tg + 2, 2:258], banks[b][:, :].rearrange("p (a b) -> p a b", b=256), AF.Copy,
                            [f'ps{b}'], ['xrp'])
                    ts(xrp[:, 1:8, 0:2], xrp[:, 0:7, 256:258], carry, None, ALU.mult, None, ['xrp', 'Vc'], ['xrp'])
                    ts(xrp[:, 0:7, 258:259], xrp[:, 1:8, 2:3], carry, None, ALU.mult, None, ['xrp', 'Vc'], ['xrp'])
                    ts(xc, xrp[:, :, 0:256], vcol('convw', 0 * 4 + c), vcol('convb', c), ALU.mult, ALU.add, ['xrp', 'Vc'], ['xc'], eng='pool')
                    for i in range(1, 4):
                        stt(xc, xrp[:, :, i:i + 256], vcol('convw', i * 4 + c), xc, ALU.mult, ALU.add, ['xrp', 'xc', 'Vc'], ['xc'])
                    act(xcb, xc2, AF.Copy, ['xc'], ['xcb'])

                def rg_mid(c):
                    A_ = [a_t, a_t2]
                    I_ = [i_t, i_t2]
                    H_ = [hf, hb]
                    AK = ['a_t', 'a_t2']
                    IK = ['i_t', 'i_t2']
                    HK = ['hf', 'hb']
                    for dr in range(2):
                        for tg in range(4):
                            b = 2 + tg % 2
                            mm(banks[b][:, :], bdm[:, (dr * 2 + 0) * 4 + c, :], xcb[:, tg * 512:(tg + 1) * 512], True, True,
                               ['bdm', 'xcb'], [f'ps{b}'])
                            act(A_[dr][:, tg * 512:(tg + 1) * 512], banks[b][:, :], AF.Sigmoid, [f'ps{b}', 'Vc'], [AK[dr]],
                                bias=vcol('ba', dr * 4 + c))
                            b2 = 4 + tg % 2
                            mm(banks[b2][:, :], bdm[:, (dr * 2 + 1) * 4 + c, :], xcb[:, tg * 512:(tg + 1) * 512], True, True,
                               ['bdm', 'xcb'], [f'ps{b2}'])
                            act(I_[dr][:, tg * 512:(tg + 1) * 512], banks[b2][:, :], AF.Sigmoid, [f'ps{b2}', 'Vc'], [IK[dr]],
                                bias=vcol('bx', dr * 4 + c))
                    for dr in range(2):
                        act(A_[dr], A_[dr], AF.Exp, [AK[dr], 'nsp'], [AK[dr]], scale=nsp[:, dr * 4 + c:dr * 4 + c + 1])
                    for dr in range(2):
                        tt(I_[dr], I_[dr], xc2, ALU.mult, [IK[dr], 'xc'], [IK[dr]], eng='pool')

                def rg_late(c):
                    A_ = [a_t, a_t2]
                    I_ = [i_t, i_t2]
                    H_ = [hf, hb]
                    AK = ['a_t', 'a_t2']
                    IK = ['i_t', 'i_t2']
                    HK = ['hf', 'hb']
                    for dr in range(2):
                        tt(H_[dr], A_[dr], A_[dr], ALU.mult, [AK[dr]], [HK[dr]])
                    for dr in range(2):
                        act(H_[dr], H_[dr], AF.Sqrt, [HK[dr]], [HK[dr]], bias=1.0, scale=-1.0)
                    for dr in range(2):
                        tt(I_[dr], I_[dr], H_[dr], ALU.mult, [IK[dr], HK[dr]], [IK[dr]])
                        h0c = vcol('h0', dr * 4 + c)
                        if dr == 0:
                            av = A_[dr].rearrange("p (s t) -> p s t", t=256)[:, 1:8, 0:1]
                            ts(av, av, carry, None, ALU.mult, None, [AK[dr], 'Vc'], [AK[dr]])
                            S.op('dve', lambda e, h0c=h0c: e.tensor_tensor_scan(out=hf, data0=a_t, data1=i_t, initial=h0c,
                                                                                op0=ALU.mult, op1=ALU.add),
                                 [AK[dr], IK[dr], 'Vc'], [HK[dr]])
                        else:
                            av = A_[dr].rearrange("p (s t) -> p s t", t=256)[:, 0:7, 255:256]
                            ts(av, av, carry, None, ALU.mult, None, [AK[dr], 'Vc'], [AK[dr]])
                            S.op('dve', lambda e, h0c=h0c: e.tensor_tensor_scan(out=hb[:, ::-1], data0=a_t2[:, ::-1],
                                                                                data1=i_t2[:, ::-1], initial=h0c,
                                                                                op0=ALU.mult, op1=ALU.add),
                                 [AK[dr], IK[dr], 'Vc'], [HK[dr]])
                    cp(stT[:, (0 * 4 + c) * 8:(0 * 4 + c) * 8 + 8], hf.rearrange("p (s t) -> p s t", t=256)[:, :, 255], ['hf'], ['stT'])
                    cp(stT[:, (1 * 4 + c) * 8:(1 * 4 + c) * 8 + 8], hb.rearrange("p (s t) -> p s t", t=256)[:, :, 0], ['hb'], ['stT'])
                    tt(hf, hf, hb, ALU.add, ['hf', 'hb'], ['hf'])
                    for tg in range(4):
                        b = 6 + tg % 2
                        for kc in range(KC):
                            mm(banks[b][:, :], wxg[:, kc, c * 128:(c + 1) * 128], hT[:, kc, tg * 512:(tg + 1) * 512],
                               kc == 0, kc == KC - 1, [wxgk] + hT_keys, [f'ps{b}'])
                        act(tmp[:, tg * 512:(tg + 1) * 512], banks[b][:, :], AF.Gelu_apprx_tanh, [f'ps{b}'], ['tmp'])
                    tt(rnnT[:, c, :], hf, tmp, ALU.mult, ['hf', 'tmp'], ['rnnT'])

                rg_front(0)
                for c in range(4):
                    rg_mid(c)
                    if c + 1 < 4:
                        rg_front(c + 1)
                    rg_late(c)
                ws.release(wxrk)
                ws.release(wxgk)
                tr(banks[0][0:64, 0:128], stT, identf, ['stT', 'identf'], ['ps0'])
                cp(stO, banks[0][0:64, 0:128], ['ps0'], ['stO'])
                for dr in range(2):
                    for c in range(4):
                        r0 = (dr * 4 + c) * 8
                        dma(st_d[:, dr * 512 + c * 128:dr * 512 + (c + 1) * 128], stO[r0:r0 + 8, :], ['stO'], ['st_d'])
                S.barrier()
                R1.release(0)
                P4.release(mrg)
                if cut == 5:
                    return
                for tc in range(NT):
                    dma(xres[:, tc, :], x_in[tc * 128:(tc + 1) * 128, :], (), [('xres', tc)])
                mixT = [attT[:, h, :] for h in range(4)] + [rnnT[:, c, :] for c in range(4)]
                mixer_epilogue(0, mixT, ['attT', 'rnnT'], ab_out_w, 'w_out0_', rows)
                S.barrier()
                P4.release(m4)

            def moe(l, last):
                S.barrier()
                ws.set_wide(True)
                m4 = P4.mark()
                g5 = P4.alloc([P, D], F32)
                lng = P4.alloc([P, D], F32)
                lnb = P4.alloc([P, D], F32)
                adaln_row(l, 5, g5, 'row_g5')
                load_row(lng, f'ln_ffn_g{l}', 'row_lng')
                load_row(lnb, f'ln_ffn_b{l}', 'row_lnb')
                idxT = P4.alloc([P, 2, NE], F32)
                gT = P4.alloc([P, 2, NE], F32)
                idxf = P4.alloc([NE, 256], F32)
                mr = P4.mark()
                xmT = [P4.alloc([P, 8, 128], BF16) for _ in range(2)]
                aff = P4.alloc([NE, T], F32)
                wS = P4.alloc([NE, T], F32)
                idxS = P4.alloc([NE, 256], F32)
                gS = P4.alloc([NE, 256], F32)
                idxP = P4.alloc([NE, 256], F32)
                gP = P4.alloc([NE, 256], F32)
                gf = P4.alloc([NE, 256], F32)
                offr = P4.alloc([NE, 256], F32)
                mi = P4.alloc([NE, 8], U32)
                def rt_A(tc):
                    i2 = tc % 2
                    bt = 4 + i2
                    for kc in range(KC):
                        tr(bank_bf(bt)[:, kc * 128:(kc + 1) * 128], xm_tok[:, tc, kc * 128:(kc + 1) * 128], identb,
                           [('xm', tc), 'identb'], [f'ps{bt}'])
                    if tc % 2 == 0:
                        cp(xmT[i2], bank_bf(bt)[:, 0:1024].rearrange("p (a b) -> p a b", b=128), [f'ps{bt}'], [f'xmT{i2}'])
                    else:
                        act(xmT[i2], bank_bf(bt)[:, 0:1024].rearrange("p (a b) -> p a b", b=128), AF.Copy, [f'ps{bt}'], [f'xmT{i2}'])

                def rt_B(tc):
                    i2 = tc % 2
                    bl = tc // 4
                    for kc in range(KC):
                        mm(banks[bl][0:NE, (tc % 4) * 128:(tc % 4 + 1) * 128], rw[:, l, kc, :], xmT[i2][:, kc, :], kc == 0, kc == KC - 1,
                           [f'xmT{i2}', 'rw'], [f'ps{bl}'])

                rt_A(0)
                for tc in range(NT):
                    if tc + 1 < NT:
                        rt_A(tc + 1)
                    rt_B(tc)
                for bl in range(4):
                    act(aff[:, bl * 512:(bl + 1) * 512], banks[bl][0:NE, :], AF.Exp, [f'ps{bl}'], ['aff'])
                for bl in range(4):
                    mm(banks[bl][0:NE, :], onesf[0:NE, 0:NE], aff[:, bl * 512:(bl + 1) * 512], True, True, ['aff', 'onesf'], [f'ps{bl}'])
                    S.op('dve', lambda e, bl=bl: e.reciprocal(out=wS[:, bl * 512:(bl + 1) * 512], in_=banks[bl][0:NE, :]), [f'ps{bl}'], ['wS'])
                tt(aff, aff, wS, ALU.mult, ['aff', 'wS'], ['aff'])
                S.op('pool', lambda e: e.iota(offr, [[256, 8], [0, 32]], base=0, channel_multiplier=0,
                                              allow_small_or_imprecise_dtypes=True), (), ['offr'])
                S.barrier()
                w32 = wS[0:32, 0:1024] if False else P4.alloc([32, 1024], F32)
                g32 = P4.alloc([32, 256], F32)
                i32f = P4.alloc([32, 256], F32)
                g1 = P4.alloc([NE, 256], F32)
                i1 = P4.alloc([NE, 256], F32)
                m0 = P4.alloc([NE, 256], F32)
                mi32 = P4.alloc([32, 256], U32)
                offc = P4.alloc([32, 1], F32)
                w128 = P4.alloc([P, 256], F32)
                g128 = P4.alloc([P, 32], F32)
                i128 = P4.alloc([P, 32], F32)
                mi128 = P4.alloc([P, 32], U32)
                for s_ in range(8):
                    dma(w128[16 * s_:16 * s_ + 16, :], aff[:, 256 * s_:256 * s_ + 256], ['aff'], ['w128'])
                cp(w32[0:NE, :], aff[:, 0:1024], ['aff'], ['w32'])
                dma(w32[NE:32, :], aff[:, 1024:2048], ['aff'], ['w32'])
                ts(offc, tcol[0:32, 0:1], 16.0, 1024.0, ALU.is_ge, ALU.mult, ['tcol'], ['offc'])
                for r in range(32):
                    sl = slice(8 * r, 8 * r + 8)
                    S.op('dve', lambda e, sl=sl: e.max(out=g32[:, sl], in_=w32), ['w32'], ['g32'])
                    S.op('dve', lambda e, sl=sl: e.max_index(out=mi32[:, sl], in_max=g32[:, sl], in_values=w32), ['w32', 'g32'], ['mi32'])
                    S.op('dve', lambda e, sl=sl: e.match_replace(out=w32, in_to_replace=g32[:, sl], in_values=w32, imm_value=0.0),
                         ['w32', 'g32'], ['w32'])
                cp(i32f, mi32, ['mi32'], ['i32f'])
                ts(i32f, i32f, offc, None, ALU.add, None, ['i32f', 'offc'], ['i32f'])
                dma(g1, g32[NE:32, :], ['g32'], ['g1'])
                dma(i1, i32f[NE:32, :], ['i32f'], ['i1'])
                g0 = g32[0:NE, :]
                i0 = i32f[0:NE, :]
                tt(m0, g0, g1[:, ::-1], ALU.is_ge, ['g32', 'g1'], ['m0'])
                tt(gS, g0, g1[:, ::-1], ALU.max, ['g32', 'g1'], ['gS'])
                tt(idxS, i0, i1[:, ::-1], ALU.subtract, ['i32f', 'i1'], ['idxS'])
                tt(idxS, idxS, m0, ALU.mult, ['idxS', 'm0'], ['idxS'])
                tt(idxS, idxS, i1[:, ::-1], ALU.add, ['idxS', 'i1'], ['idxS'])
                for r in range(4):
                    sl = slice(8 * r, 8 * r + 8)
                    S.op('dve', lambda e, sl=sl: e.max(out=g128[:, sl], in_=w128), ['w128'], ['g128'])
                    S.op('dve', lambda e, sl=sl: e.max_index(out=mi128[:, sl], in_max=g128[:, sl], in_values=w128), ['w128', 'g128'], ['mi128'])
                    S.op('dve', lambda e, sl=sl: e.match_replace(out=w128, in_to_replace=g128[:, sl], in_values=w128, imm_value=0.0),
                         ['w128', 'g128'], ['w128'])
                cp(i128, mi128, ['mi128'], ['i128'])
                for s_ in range(8):
                    dma(gP[:, 32 * s_:32 * s_ + 32], g128[16 * s_:16 * s_ + 16, :], ['g128'], ['gP'])
                    dma(idxP[:, 32 * s_:32 * s_ + 32], i128[16 * s_:16 * s_ + 16, :], ['i128'], ['idxP'])
                tt(idxP, idxP, offr, ALU.add, ['idxP', 'offr'], ['idxP'])
                ts(idxf, idxS, fS[0:NE], None, ALU.mult, None, ['idxS', 'Vc'], ['idxf'])
                stt(idxf, idxP, fP[0:NE], idxf, ALU.mult, ALU.add, ['idxP', 'idxf', 'Vc'], ['idxf'])
                ts(gf, gS, fS[0:NE], None, ALU.mult, None, ['gS', 'Vc'], ['gf'])
                stt(gf, gP, fP[0:NE], gf, ALU.mult, ALU.add, ['gP', 'gf', 'Vc'], ['gf'])
                for jc in range(2):
                    tr(banks[0][:, jc * 16:(jc + 1) * 16], idxf[:, jc * 128:(jc + 1) * 128], identf[0:NE, 0:NE], ['idxf', 'identf'], ['ps0'])
                    tr(banks[0][:, 32 + jc * 16:32 + (jc + 1) * 16], gf[:, jc * 128:(jc + 1) * 128], identf[0:NE, 0:NE], ['gf', 'identf'], ['ps0'])
                cp(idxT, banks[0][:, 0:32].rearrange("p (a b) -> p a b", b=16), ['ps0'], ['idxT'])
                cp(gT, banks[0][:, 32:64].rearrange("p (a b) -> p a b", b=16), ['ps0'], ['gT'])
                S.barrier()
                P4.release(mr)
                me_ = P4.mark()
                yg = [P4.alloc([P, GRP, 2, D], BF16) for _ in range(2)]
                sel = P4.alloc([P, NT, 256], BF16)
                xgT = [P4.alloc([P, 8, 256], BF16) for _ in range(2)]
                hid = [P4.alloc([P, 4, 256], BF16) for _ in range(2)]
                s1 = [P4.alloc([P, 256], F32) for _ in range(2)]
                selT = [P4.alloc([P, GRP * 2, 128], BF16) for _ in range(2)]
                rhe = P4.alloc([NE, 256], F32)
                YB = [0, 1, 2, 3]
                sc_it = [0]

                def selT_build(grp, tc, i2):
                    sT = selT[i2]
                    for q in range(GRP * 2):
                        ee = grp * GRP + q // 2
                        jc = q % 2
                        ts(sT[:, q, :], iota128, float(128 * tc), idxT[:, jc, ee:ee + 1], ALU.add, ALU.is_equal,
                           ['iota128', 'idxT'], [f'selT{i2}'])

                def scatter_mm(grp, tc, i2):
                    ygb = yg[grp % 2]
                    ygk = f'yg{grp % 2}'
                    sT = selT[i2]
                    sk = f'selT{i2}'
                    for dh in range(2):
                        b = 6 + dh
                        rk = [f'ps{b}']
                        for q in range(GRP * 2):
                            mm(banks[b][:, :], sT[:, q, :], ygb[:, q // 2, q % 2, dh * 512:(dh + 1) * 512], q == 0, q == GRP * 2 - 1,
                               [sk, ygk], rk)
                        xs = xres[:, tc, dh * 512:(dh + 1) * 512]
                        tt(xs, xs, banks[b][:, :], ALU.add, rk + [('xres', tc)], [('xres', tc)])

                def sel_build(e_):
                    ts(rhe, idxf, identf[0:NE, e_:e_ + 1], None, ALU.mult, None, ['idxf', 'identf'], ['rhe'])
                    mm(banks[7][:, 0:256], onesf[0:NE, :], rhe, True, True, ['rhe', 'onesf'], ['ps7'])
                    for tc in range(NT):
                        ts(sel[:, tc, :], banks[7][:, 0:256], tcol[:, tc:tc + 1], None, ALU.is_equal, None, ['ps7', 'tcol'], [('sel', tc)])

                def gather(e_):
                    xg_ = xgT[e_ % 2]
                    xgk = f'xgT{e_ % 2}'
                    for dc in range(KC):
                        b = 6 + dc % 2
                        for tc in range(NT):
                            mm(banks[b][:, 0:256], xm_tok[:, tc, dc * 128:(dc + 1) * 128], sel[:, tc, :], tc == 0, tc == NT - 1,
                               [('xm', tc), ('sel', tc)], [f'ps{b}'])
                        act(xg_[:, dc, :], banks[b][:, 0:256], AF.Copy, [f'ps{b}'], [xgk])

                for tc in range(NT):
                    act(xres[:, tc, :], xres[:, tc, :], AF.Copy, [('xres', tc)], [('xres', tc)], scale=ALPHA)
                NSLOT = GRP * 4
                TCS = NT // NSLOT
                sel_build(0)
                gather(0)
                for e_ in range(NE):
                    el = e_ % GRP
                    grp = e_ // GRP
                    ygb = yg[grp % 2]
                    ygk = f'yg{grp % 2}'
                    xg_ = xgT[e_ % 2]
                    xgk = f'xgT{e_ % 2}'
                    for fg in range(4):
                        w1v, w1k = wpiece(f'w1_{l}_{e_}_{fg}', moe_w1[l, e_], fg * 512, 512, moe=True)
                        w3v, w3k = wpiece(f'w3_{l}_{e_}_{fg}', moe_w3[l, e_], fg * 512, 512, moe=True)
                        src2 = moe_w2[l, e_, fg * 512:(fg + 1) * 512, :].rearrange("(fc p) d -> p fc d", p=128)
                        w2v, w2k = ws.next(f'w2_{l}_{e_}_{fg}', src2, [P, 4, D], True)
                        hd = hid[fg % 2]
                        hk = f'hid{fg % 2}'
                        slot = el * 4 + fg
                        if grp >= 1:
                            for i2 in range(TCS):
                                selT_build(grp - 1, slot * TCS + i2, i2)
                        for fc in range(4):
                            b = 4 + fc % 2
                            for dc in range(KC):
                                mm(banks[b][:, 0:256], w1v[:, dc, fc * 128:(fc + 1) * 128], xg_[:, dc, :], dc == 0, dc == KC - 1,
                                   [w1k, xgk], [f'ps{b}'])
                            for dc in range(KC):
                                mm(banks[b][:, 256:512], w3v[:, dc, fc * 128:(fc + 1) * 128], xg_[:, dc, :], dc == 0, dc == KC - 1,
                                   [w3k, xgk], [f'ps{b}'])
                            s_ = s1[fc % 2]
                            act(s_, banks[b][:, 0:256], AF.Silu, [f'ps{b}'], [f's1{fc % 2}'])
                            tt(hd[:, fc, :], s_, banks[b][:, 256:512], ALU.mult, [f'ps{b}', f's1{fc % 2}'], [hk])
                        for jc in range(2):
                            if grp >= 1:
                                scatter_mm(grp - 1, slot * TCS + jc, jc)
                            for dh in range(2):
                                yb = YB[jc * 2 + dh]
                                for fc in range(4):
                                    mm(banks[yb][:, :], hd[:, fc, jc * 128:(jc + 1) * 128], w2v[:, fc, dh * 512:(dh + 1) * 512],
                                       fg == 0 and fc == 0, fg == 3 and fc == 3, [hk, w2k], [f'ps{yb}'])
                        ws.release(w1k)
                        ws.release(w3k)
                        ws.release(w2k)
                        if e_ + 1 < NE:
                            if fg == 1:
                                sel_build(e_ + 1)
                            if fg == 2:
                                gather(e_ + 1)
                    for jc in range(2):
                        for dh in range(2):
                            yb = YB[jc * 2 + dh]
                            stt(ygb[:, el, jc, dh * 512:(dh + 1) * 512], banks[yb][:, :], gT[:, jc, e_:e_ + 1], g5[:, dh * 512:(dh + 1) * 512],
                                ALU.mult, ALU.mult, [f'ps{yb}', 'gT', 'row_g5'], [ygk])
                S.barrier()
                stt_ = P4.alloc([P, 32], F32)
                selflat = sel.rearrange("p a b -> p (a b)").bitcast(F32)
                xn_ = [selflat[:, 0:D], selflat[:, D:2 * D]]
                if not last:
                    adaln_cols(l + 1)
                selT_build(NE // GRP - 1, 0, 0)
                for tc in range(NT):
                    if tc + 1 < NT:
                        selT_build(NE // GRP - 1, tc + 1, (tc + 1) % 2)
                    scatter_mm(NE // GRP - 1, tc, tc % 2)
                    layer_norm_chunk(tc, stt_, xn_[tc % 2], lng, lnb, 'row_lng', 'row_lnb', tail_eng='pool')
                    if not last and tc % 4 == 3:
                        make_hT(l + 1, lambda t: (xres[:, t, :], ('xres', t)), g4s=(tc // 4,))
                    if last:
                        dma(y_d[tc * 128:(tc + 1) * 128, :], xres[:, tc, :], [('xres', tc)], ['y_d'])
                S.barrier()
                P4.release(m4)
                ws.set_wide(False)

            def layer1_mixer():
                S.barrier()
                m4 = P4.mark()
                uT = P4.alloc([P, 8, T], BF16)
                rowsA = [B2.alloc([P, D], F32) for _ in range(4)]
                mg = P4.mark()
                spw = P4.alloc([P, 8, 128], BF16)
                dma(spw, spwT_d[:, :].rearrange("p (g q) -> p g q", q=128), (), ['spw'], q='pool')
                for half in range(2):
                    wv, wk = wpiece(f'w_u{half}', sgu_in_w, half * 512, 512)
                    for oc in range(4):
                        for tg in range(4):
                            b = (oc * 4 + tg) % 4
                            for kc in range(KC):
                                mm(banks[b][:, :], wv[:, kc, oc * 128:(oc + 1) * 128], hT[:, kc, tg * 512:(tg + 1) * 512],
                                   kc == 0, kc == KC - 1, [wk] + hT_keys, [f'ps{b}'])
                            act(uT[:, half * 4 + oc, tg * 512:(tg + 1) * 512], banks[b][:, :], AF.Gelu_apprx_tanh, [f'ps{b}', 'Vc'], ['uT'],
                                bias=vcol('sgub_u', half * 4 + oc))
                    ws.release(wk)
                vb, lg, lb, spb = rowsA[0], rowsA[1], rowsA[2], rowsA[3]
                load_row(vb, 'sgub_v', 'row_vb')
                load_row(lg, 'sgu_ln_g', 'row_lg')
                load_row(lb, 'sgu_ln_b', 'row_lb')
                load_row(spb, 'spb', 'row_spb')
                wvv = [wpiece(f'w_v{half}', sgu_in_w, 1024 + half * 512, 512) for half in range(2)]
                vt = [P4.alloc([P, D], F32) for _ in range(2)]
                vnb = [P4.alloc([P, D], BF16) for _ in range(2)]
                st_ = P4.alloc([P, 32], F32)
                mxs = P4.alloc([P, D], F32)
                mxs2 = [mxs, P4.alloc([P, D], F32)]

                def sgu_A(tc):
                    i2 = tc % 2
                    v_ = vt[i2]
                    vk = f'vt{i2}'
                    for half in range(2):
                        b = half
                        for kc in range(KC):
                            mm(banks[b][:, :], hT[:, kc, tc * 128:(tc + 1) * 128], wvv[half][0][:, kc, :], kc == 0, kc == KC - 1,
                               [wvv[half][1]] + hT_keys, [f'ps{b}'])
                        tt(v_[:, half * 512:(half + 1) * 512], banks[b][:, :], vb[:, half * 512:(half + 1) * 512], ALU.add,
                           [f'ps{b}', 'row_vb'], [vk])
                    act(v_, v_, AF.Gelu_apprx_tanh, [vk], [vk])
                    sx = st_[:, 16 * i2:16 * i2 + 16]
                    S.op('dve', lambda e, v_=v_, sx=sx: e.bn_stats(out=sx[:, 0:6], in_=v_[:, 0:512]), [vk], [f'vst{i2}'])
                    S.op('dve', lambda e, v_=v_, sx=sx: e.bn_stats(out=sx[:, 6:12], in_=v_[:, 512:1024]), [vk], [f'vst{i2}'])
                    S.op('dve', lambda e, sx=sx: e.bn_aggr(out=sx[:, 12:14], in_=sx[:, 0:12]), [f'vst{i2}'], [f'vmv{i2}'])
                    ts(sx[:, 14:15], sx[:, 13:14], LN_EPS, None, ALU.add, None, [f'vmv{i2}'], [f'vr{i2}'], eng='pool')
                    tt(sx[:, 14:15], sx[:, 14:15], small[:, 0:1], ALU.pow, [f'vr{i2}', 'mhalf'], [f'vr{i2}'], eng='pool')
                    ts(sx[:, 15:16], sx[:, 12:13], sx[:, 14:15], -1.0, ALU.mult, ALU.mult, [f'vmv{i2}', f'vr{i2}'], [f'vn{i2}'])
                    act(v_, v_, AF.Identity, [vk, f'vr{i2}', f'vn{i2}'], [vk], bias=sx[:, 15:16], scale=sx[:, 14:15])
                    tt(v_, v_, lg, ALU.mult, [vk, 'row_lg'], [vk])
                    tt(vnb[i2], v_, lb, ALU.add, [vk, 'row_lb'], [f'vnb{i2}'], eng='pool')

                def sgu_B(tc):
                    i2 = tc % 2
                    mx_ = mxs2[i2]
                    for g in range(8):
                        b = 2 + g // 4 + 2 * i2
                        mm(banks[b][:, (g % 4) * 128:(g % 4 + 1) * 128], vnb[i2][:, g * 128:(g + 1) * 128], spw[:, g, :], True, True,
                           [f'vnb{i2}', 'spw'], [f'ps{b}'])
                    for hh in range(2):
                        b = 2 + hh + 2 * i2
                        tt(mx_[:, hh * 512:(hh + 1) * 512], banks[b][:, :], spb[:, hh * 512:(hh + 1) * 512], ALU.add, [f'ps{b}', 'row_spb'],
                           [f'mxs{i2}'])
                    ug = uT[:, :, tc * 128:(tc + 1) * 128]
                    tt(ug, ug, mx_.rearrange("p (g q) -> p g q", q=128), ALU.mult, ['uT', f'mxs{i2}'], ['uT'])

                sgu_A(0)
                for tc in range(NT):
                    if tc + 1 < NT:
                        sgu_A(tc + 1)
                    sgu_B(tc)
                ws.release(wvv[0][1])
                ws.release(wvv[1][1])
                S.barrier()
                P4.release(mg)
                rows = dict(g2=rowsA[0], lng=rowsA[1], lnb=rowsA[2], s4p=rowsA[3], s3=P4.alloc([P, D], F32))
                adaln_row(1, 2, rows['g2'], 'row_g2')
                adaln_row(1, 3, rows['s3'], 'row_s3')
                adaln_row(1, 4, rows['s4p'], 'row_s4p', plus1=True)
                load_row(rows['lng'], 'ln_mix_g1', 'row_lng')
                load_row(rows['lnb'], 'ln_mix_b1', 'row_lnb')
                mixer_epilogue(1, [uT[:, kc, :] for kc in range(8)], ['uT'], sgu_out_w, 'w_out1_', rows)
                S.barrier()
                P4.release(m4)

            B2 = Arena(arena_t, slot4_base, 2 * 8192)

            layer0_mixer()
            if stage >= 2:
                moe(0, last=False)
            if stage >= 3:
                B2.release(0)
                layer1_mixer()
            if stage >= 4:
                moe(1, last=True)
            if stage < 4:
                S.barrier()
                for tc in range(NT):
                    dma(y_d[tc * 128:(tc + 1) * 128, :], xres[:, tc, :], [('xres', tc)], ['y_d'])
            S.barrier()


        S1 = Sched()
        ws1 = WStream(S1, slots, plan=None)
        tops = (A.top, P4.top, R1.top)
        emit_all(S1, ws1)
        plan = ws1.rec
        A.top, P4.top, R1.top = tops
        S2 = Sched()
        ws2 = WStream(S2, slots, plan=plan)
        emit_all(S2, ws2)
        print(f"[kernel] ops={S2.nops} pieces={len(plan)} sems={S2.cnt}", flush=True)
        S2.replay(nc, es)
    return nc


def _prep_shared(inp):
    f = lambda k: np.ascontiguousarray(np.asarray(inp[k], dtype=np.float32))
    sh = {}
    sh["mod_w"] = f("mod_w")
    w = f("ab_in_w")[0]
    swp = np.arange(128) ^ 1
    q = w[:, 0:512].reshape(D, 4, 128)
    k = w[:, 512:768].reshape(D, 2, 128)
    sh["w_in"] = np.ascontiguousarray(np.concatenate([w, q[:, :, swp].reshape(D, 512), k[:, :, swp].reshape(D, 256)], axis=1))
    sh["ab_out_w"] = f("ab_out_w")[0]
    bd = np.zeros((2, 2, 4, 128, 128), np.float32)
    wa, wx = f("lru_wa")[0], f("lru_wx")[0]
    for dr in range(2):
        for gi, wsrc in enumerate((wa, wx)):
            for blk in range(8):
                c, o = blk // 2, (blk % 2) * 64
                bd[dr, gi, c, o:o + 64, o:o + 64] = wsrc[dr, blk]
    sh["bd"] = bd.reshape(16, 128, 128)
    sh["sgu_in_w"] = f("sgu_in_w")[0]
    sh["spwT"] = np.ascontiguousarray(f("sgu_spatial_w")[0].transpose(2, 0, 1).reshape(128, 1024))
    sh["sgu_out_w"] = f("sgu_out_w")[0]
    sh["router_w"] = f("router_w")
    sh["moe_w1"] = f("moe_w1")
    sh["moe_w3"] = f("moe_w3")
    sh["moe_w2"] = f("moe_w2")
    return sh


def _rope_tables():
    t = np.arange(T)
    row = (t // 64).astype(np.float32)
    col = (t % 64).astype(np.float32)
    nf = 32
    freqs = (np.float32(10000.0) ** (-np.arange(nf, dtype=np.float32) / np.float32(nf))).astype(np.float32)
    ang = np.concatenate([row[:, None] * freqs, col[:, None] * freqs], axis=-1).astype(np.float32)
    cos, sin = np.cos(ang), np.sin(ang)
    dd = np.arange(128)
    C = cos[:, dd // 2].T
    Ss = sin[:, dd // 2].T * np.where(dd % 2 == 0, -1.0, 1.0)[:, None]
    return np.ascontiguousarray(np.concatenate([C, Ss], axis=1).astype(np.float32))


def _masks(is_sample):
    m = np.zeros((128, 16, 2, 128), np.float32)
    kk = np.arange(128)[:, None]
    qq = np.arange(128)[None, :]
    for n in range(16):
        if is_sample:
            m[:, n, 0, :] = np.where(kk >= qq, 0.0, NEGM) if n >= 1 else NEGM
            m[:, n, 1, :] = np.where(kk <= qq, 0.0, NEGM) if n <= 14 else NEGM
        else:
            m[:, n, 0, :] = 0.0 if n % 2 == 1 else NEGM
            m[:, n, 1, :] = 0.0 if n % 2 == 0 else NEGM
    return np.ascontiguousarray(m.reshape(128, -1))


def _prep_core(inp, core):
    f = lambda k: np.asarray(inp[k], dtype=np.float32)
    is_s = core < 4
    m = {}
    if is_s:
        m["x_in"] = np.ascontiguousarray(f("x_sample")[core])
        cv = f("c")[core]
    else:
        m["x_in"] = np.ascontiguousarray(f("x_prompt")[8 * (core - 4):8 * (core - 3)].reshape(T, D))
        cv = f("c_ctx")
    V = np.zeros((NV, 128), np.float32)

    def put(name, arr):
        b, n = VR[name]
        V[b:b + n] = np.asarray(arr, np.float32).reshape(n, 128)
    put('c', cv)
    mb = f("mod_b")
    for l in range(2):
        put(f'modb{l}_0', mb[l, 0:1024])
        put(f'modb{l}_1', mb[l, 1024:2048])
    put('convw', f("rnn_conv_w")[0])
    put('convb', f("rnn_conv_b")[0])
    put('ba', f("lru_ba")[0])
    put('bx', f("lru_bx")[0])
    put('lam', f("lru_lambda")[0])
    put('h0', f("state_rglru")[core, 0] if is_s else np.zeros((2, 512), np.float32))
    put('sgub_u', f("sgu_in_b")[0, 0:1024])
    fl = np.zeros((4, 128), np.float32)
    fl[0] = 1.0 if is_s else 0.0
    fl[1] = 0.0 if is_s else NEGM
    fl[2] = 1.0 if is_s else 0.0
    fl[3] = 0.0 if is_s else 1.0
    put('flags', fl)
    put('sink', np.repeat(f("attn_sink")[0][:, None], 128, axis=1))
    m["vecs"] = V
    Rw = np.zeros((NR, D), np.float32)
    for l in range(2):
        for i in range(2, 6):
            Rw[RR[f'modb{l}_{i}']] = mb[l, i * 1024:(i + 1) * 1024]
        for nme in ('ln_mix_g', 'ln_mix_b', 'ln_ffn_g', 'ln_ffn_b'):
            Rw[RR[f'{nme}{l}']] = f(nme)[l]
    Rw[RR['sgub_v']] = f("sgu_in_b")[0, 1024:2048]
    Rw[RR['sgu_ln_g']] = f("sgu_ln_g")[0]
    Rw[RR['sgu_ln_b']] = f("sgu_ln_b")[0]
    Rw[RR['spb']] = f("sgu_spatial_b")[0].reshape(-1)
    m["rowsrc"] = Rw
    if is_s:
        ck = f("cache_k")[core, 0]
        cvv = f("cache_v")[core, 0]
        m["kctxT"] = np.ascontiguousarray(ck.transpose(2, 1, 0).reshape(128, 512))
        m["vctx"] = np.ascontiguousarray(cvv.reshape(2, 128, 256).transpose(1, 0, 2).reshape(128, 512))
        m["rope"] = _rope_tables()
    else:
        m["kctxT"] = np.zeros((128, 512), np.float32)
        m["vctx"] = np.zeros((128, 512), np.float32)
        m["rope"] = np.ascontiguousarray(np.concatenate([np.ones((128, T), np.float32), np.zeros((128, T), np.float32)], axis=1))
    m["maskLR"] = _masks(is_s)
    return m


_CACHE = {}


def kernel(**inputs):
    stage = int(os.environ.get("MK_STAGE", "99"))
    cut = int(os.environ.get("MK_CUT", "99"))
    cores = [int(c) for c in os.environ.get("MK_CORES", "0,1,2,3,4,5,6,7").split(",")]
    if (stage, cut) not in _CACHE:
        _CACHE[(stage, cut)] = build_program(stage, cut)
    nc = _CACHE[(stage, cut)]
    sh = _prep_shared(inputs)
    if stage < 2:
        for k in ("moe_w1", "moe_w3", "moe_w2"):
            sh.pop(k)
    in_maps = []
    for core in cores:
        m = _prep_core(inputs, core)
        m.update(sh)
        in_maps.append(m)
    res = run_bass_kernel_spmd(nc, in_maps, core_ids=list(range(len(cores))))
    R = res.results
    if len(cores) < 8:
        R = {c: R[i] for i, c in enumerate(cores)}
        for c in range(8):
            R.setdefault(c, R[cores[0]] if c < 4 else R[cores[-1]])
    y_sample = np.stack([R[c]["y"] for c in range(4)], 0).astype(np.float32)
    y_prompt = np.concatenate([R[c]["y"].reshape(8, 256, D) for c in range(4, 8)], 0).astype(np.float32)
    kv = np.concatenate([R[c]["kv_out"].reshape(8, 256, 512) for c in range(4, 8)], 0)
    new_k = np.ascontiguousarray(kv[:, :, 0:256].reshape(32, 1, 256, 2, 128)).astype(np.float32)
    new_v = np.ascontiguousarray(kv[:, :, 256:512].reshape(32, 1, 256, 2, 128)).astype(np.float32)
    st = np.concatenate([R[c]["st_out"].reshape(8, 1, 2, 512) for c in range(4, 8)], 0).astype(np.float32)
    return (y_prompt, y_sample, new_k, new_v, st)
```

```python
import os
import numpy as np
from contextlib import ExitStack
import concourse.bass as bass
import concourse.mybir as mybir
from concourse.bass_utils import run_bass_kernel_spmd

F32 = mybir.dt.float32
BF16 = mybir.dt.bfloat16
U32 = mybir.dt.uint32
AF = mybir.ActivationFunctionType
ALU = mybir.AluOpType

P = 128
T = 2048
D = 1024
NT = 16
KC = 8
NEGM = -30000.0
ALPHA = 4.0 ** 0.25
LN_EPS = 1e-6
ATTN_SCALE = 128.0 ** -0.5
NE = 16
ROPE_ENG = 'dve' if os.environ.get('MK_DBG') == 'e' else 'pool'
GRP = 2

VR = {}


def _reg(name, n):
    VR[name] = (len_vr[0], n)
    len_vr[0] += n


len_vr = [0]
_reg('c', 8)
for _l in range(2):
    _reg(f'modb{_l}_0', 8)
    _reg(f'modb{_l}_1', 8)
_reg('convw', 16)
_reg('convb', 4)
_reg('ba', 8)
_reg('bx', 8)
_reg('lam', 8)
_reg('h0', 8)
_reg('sgub_u', 8)
_reg('flags', 4)
_reg('sink', 4)
NV = len_vr[0]
assert NV <= 128

RR = {}
_r = 0
for _l in range(2):
    for _i in range(2, 6):
        RR[f'modb{_l}_{_i}'] = _r
        _r += 1
    for _n in ('ln_mix_g', 'ln_mix_b', 'ln_ffn_g', 'ln_ffn_b'):
        RR[f'{_n}{_l}'] = _r
        _r += 1
for _n in ('sgub_v', 'sgu_ln_g', 'sgu_ln_b', 'spb'):
    RR[_n] = _r
    _r += 1
NR = _r


class Sched:
    ENG = ['pe', 'act', 'dve', 'pool', 'sp']

    def __init__(self):
        self.prog = {e: [] for e in self.ENG}
        self.cnt = {}
        self.seen = {e: {} for e in self.ENG}
        self.lastw = {}
        self.readers = {}
        self.nops = 0
        self.dmaidx = {}

    def _need(self, eng, dep):
        sem, val = dep[0], dep[1]
        if self.seen[eng].get(sem, 0) >= val:
            return
        self.seen[eng][sem] = val
        self.prog[eng].append(('wait', sem, val))

    def op(self, eng, emit, reads=(), writes=(), dma=False, semkey=None):
        if dma:
            if semkey is None:
                j = self.dmaidx.get(eng, 0)
                self.dmaidx[eng] = j + 1
                semkey = f'{eng}{j % 8}'
            sem = 'd_' + semkey
            if self.cnt.get(sem, 0) > 0:
                self._need(eng, (sem, self.cnt[sem]))
        else:
            sem = 's_' + eng
        bx = set()
        for k in list(reads) + list(writes):
            kb = k[0] if isinstance(k, tuple) else k
            if isinstance(kb, str) and kb[:2] == 'ps' and kb[2:].isdigit():
                bx.add(('bankx', kb))
        if bx:
            writes = list(writes) + sorted(bx)
        for k in reads:
            w = self.lastw.get(k)
            if w:
                self._need(eng, w)
        for k in writes:
            w = self.lastw.get(k)
            if w and (w[2] != eng or dma or w[0][0] == 'd'):
                self._need(eng, w)
            for r in self.readers.get(k, ()):
                if r[2] != eng or eng != 'pe' or dma or r[0][0] == 'd':
                    self._need(eng, r)
        inc = 16 if dma else 1
        self.cnt[sem] = self.cnt.get(sem, 0) + inc
        me = (sem, self.cnt[sem], eng)
        self.prog[eng].append(('op', emit, sem, inc))
        for k in reads:
            self.readers.setdefault(k, []).append(me)
        for k in writes:
            self.lastw[k] = me
            self.readers[k] = []
        self.nops += 1

    def barrier(self):
        for e in self.ENG:
            for sem, v in self.cnt.items():
                self._need(e, (sem, v))
        self.lastw = {}
        self.readers = {}

    def replay(self, nc, es):
        sems = {sem: es.enter_context(nc.semaphore(sem)) for sem in self.cnt}
        block = es.enter_context(nc.Block())

        def run(name):
            def f(e):
                for it in self.prog[name]:
                    if it[0] == 'wait':
                        e.wait_ge(sems[it[1]], it[2])
                    else:
                        it[1](e).then_inc(sems[it[2]], it[3])
            return f
        block.tensor(run('pe'))
        block.scalar(run('act'))
        block.vector(run('dve'))
        block.gpsimd(run('pool'))
        block.sync(run('sp'))


class Arena:
    def __init__(self, tensor, base, size):
        self.t, self.base, self.size, self.top = tensor, base, size, 0

    def alloc(self, shape, dt):
        bpe = 2 if dt == BF16 else 4
        n = int(np.prod(shape[1:]))
        off = (self.top + 63) // 64 * 64
        nb = n * bpe
        assert off + nb <= self.size, f"arena overflow {off + nb} > {self.size}"
        self.top = off + nb
        b = (self.base + off) // 2
        ap = self.t[:, b:b + nb // 2]
        if dt != BF16:
            ap = ap.bitcast(dt)
        if len(shape) == 3:
            ap = ap.rearrange("p (a b) -> p a b", b=shape[2])
        elif len(shape) == 4:
            ap = ap.rearrange("p (a b c) -> p a b c", b=shape[2], c=shape[3])
        if shape[0] != P:
            ap = ap[0:shape[0]]
        return ap

    def mark(self):
        return self.top

    def release(self, m):
        self.top = m

    def sub(self, size):
        off = (self.top + 63) // 64 * 64
        assert off + size <= self.size
        self.top = off + size
        return Arena(self.t, self.base + off, size)


class WStream:
    def __init__(self, S, slots, plan=None):
        self.S, self.slots, self.plan = S, slots, plan
        self.rec = []
        self.i = 0
        self.issued = 0
        self.free = [0, 1, 2, 3]
        self.slot_of = {}
        self.wide = False
        self.prev = None

    def set_wide(self, wide):
        self.wide = wide
        if wide:
            for s in (4, 5):
                if s not in self.free:
                    self.free.append(s)
            self._issue()
        else:
            self.free = [s for s in self.free if s < 4]

    def _issue(self):
        if self.plan is None:
            return
        while self.issued < len(self.plan) and self.free:
            tag, src, shape, moe = self.plan[self.issued]
            s = None
            for cand in self.free:
                if moe or cand < 4:
                    s = cand
                    break
            if s is None:
                break
            self.free.remove(s)
            dst = self._view(s, shape)
            self.S.op('pool', lambda e, dst=dst, src=src: e.dma_start(out=dst, in_=src),
                      reads=(), writes=[f'slot{s}'], dma=True, semkey=f'slot{s}')
            self.slot_of[self.issued] = s
            self.issued += 1

    def _view(self, s, shape):
        n = int(np.prod(shape[1:]))
        ap = self.slots[s][:, 0:n]
        if len(shape) == 3:
            ap = ap.rearrange("p (a b) -> p a b", b=shape[2])
        return ap

    def next(self, tag, src, shape, moe=False):
        if self.plan is None:
            self.rec.append((tag, src, shape, moe))
            return self._view(len(self.rec) % 4, shape), f'slot{len(self.rec) % 4}'
        assert self.plan[self.i][0] == tag, (self.plan[self.i][0], tag)
        self._issue()
        assert self.i in self.slot_of, f"weight stream starved (no slot) at {tag}"
        s = self.slot_of.pop(self.i)
        self.i += 1
        return self._view(s, shape), f'slot{s}'

    def release(self, key):
        if self.plan is None:
            return
        s = int(key[4:])
        if s < 4 or self.wide:
            self.free.append(s)
        self._issue()


def build_program(stage=99, cut=99):
    nc = bass.Bass("TRN2", target_bir_lowering=False)

    def din(name, shape):
        return nc.dram_tensor(name, list(shape), F32, kind="ExternalInput").ap()

    x_in = din("x_in", [T, D])
    vecs = din("vecs", [NV, 128])
    rowsrc = din("rowsrc", [NR, D])
    kctx_d = din("kctxT", [128, 512])
    vctx_d = din("vctx", [128, 512])
    mask_d = din("maskLR", [128, 16 * 2 * 128])
    rope_d = din("rope", [128, 2 * T])
    mod_w = din("mod_w", [2, D, 6 * D])
    w_in = din("w_in", [D, 2816])
    ab_out_w = din("ab_out_w", [D, D])
    bd_d = din("bd", [16, 128, 128])
    sgu_in_w = din("sgu_in_w", [D, 2 * D])
    spwT_d = din("spwT", [128, 8 * 128])
    sgu_out_w = din("sgu_out_w", [D, D])
    router_w = din("router_w", [2, D, NE])
    if stage >= 2:
        moe_w1 = din("moe_w1", [2, NE, D, 2 * D])
        moe_w3 = din("moe_w3", [2, NE, D, 2 * D])
        moe_w2 = din("moe_w2", [2, NE, 2 * D, D])
    y_d = nc.dram_tensor("y", [T, D], F32, kind="ExternalOutput").ap()
    kv_d = nc.dram_tensor("kv_out", [T, 512], F32, kind="ExternalOutput").ap()
    st_d = nc.dram_tensor("st_out", [8, 1024], F32, kind="ExternalOutput").ap()

    es = ExitStack()
    with es:
        ARENA_BYTES = 211968
        arena_t = es.enter_context(nc.sbuf_tensor("arena", [P, ARENA_BYTES // 2], BF16))
        banks = [es.enter_context(nc.psum_tensor(f"ps{i}", [P, 512], F32)) for i in range(8)]
        A = Arena(arena_t, 0, ARENA_BYTES)
        identf = A.alloc([P, 128], F32)
        identb = A.alloc([P, 128], BF16)
        onesb = A.alloc([P, 128], BF16)
        onesf = A.alloc([P, 128], F32)
        iota128 = A.alloc([P, 128], F32)
        tcol = A.alloc([P, 16], F32)
        Vc = A.alloc([P, NV], F32)
        sc = A.alloc([P, 8], BF16)
        scB = A.alloc([P, 8, 128], BF16)
        modc = A.alloc([P, 2, 16], F32)
        esink = A.alloc([P, 4], F32)
        nsp = A.alloc([P, 8], F32)
        rw = A.alloc([P, 2, 8, 16], BF16)
        small = A.alloc([P, 64], F32)
        xres = A.alloc([P, NT, D], F32)
        P1_base = A.top - NT * D * 4
        p2 = A.alloc([P, 16384], BF16)
        hT = p2.rearrange("p (a b) -> p a b", b=T)
        xm_tok = p2.rearrange("p (a b) -> p a b", b=D)
        slots = [A.alloc([P, 4096], BF16) for _ in range(4)]
        slot4_base = (A.top + 63) // 64 * 64
        slots += [A.alloc([P, 4096], BF16) for _ in range(2)]
        assert A.top == slot4_base + 16384
        P4 = A.sub(ARENA_BYTES - ((A.top + 63) // 64 * 64))
        R1 = Arena(arena_t, (P1_base + 63) // 64 * 64, NT * D * 4 - 64)

        def vcol(name, j=0, n=1):
            b = VR[name][0] + j
            return Vc[:, b:b + n]

        def emit_all(S, ws):
            def mm(out, lhsT, rhs, start, stop, reads, writes):
                S.op('pe', lambda e: e.matmul(out, lhsT, rhs, start=start, stop=stop), reads, writes)

            def tr(out, in_, ident, reads, writes):
                S.op('pe', lambda e: e.transpose(out, in_, ident), reads, writes)

            def act(out, in_, func, reads, writes, bias=None, scale=None):
                kw = {}
                if bias is not None:
                    kw['bias'] = bias
                if scale is not None:
                    kw['scale'] = scale
                S.op('act', lambda e: e.activation(out=out, in_=in_, func=func, **kw), reads, writes)

            def tt(out, in0, in1, op, reads, writes, eng='dve'):
                S.op(eng, lambda e: e.tensor_tensor(out=out, in0=in0, in1=in1, op=op), reads, writes)

            def ts(out, in0, s1, s2, op0, op1, reads, writes, eng='dve'):
                if s2 is None:
                    S.op(eng, lambda e: e.tensor_scalar(out=out, in0=in0, scalar1=s1, scalar2=None, op0=op0), reads, writes)
                else:
                    S.op(eng, lambda e: e.tensor_scalar(out=out, in0=in0, scalar1=s1, scalar2=s2, op0=op0, op1=op1), reads, writes)

            def stt(out, in0, scalar, in1, op0, op1, reads, writes, eng='dve'):
                S.op(eng, lambda e: e.scalar_tensor_tensor(out=out, in0=in0, scalar=scalar, in1=in1, op0=op0, op1=op1), reads, writes)

            def cp(out, in_, reads, writes, eng='dve'):
                S.op(eng, lambda e: e.tensor_copy(out=out, in_=in_), reads, writes)

            def dma(out, in_, reads, writes, q='sp'):
                S.op(q, lambda e: e.dma_start(out=out, in_=in_), reads, writes, dma=True)

            def bank_bf(i):
                return banks[i][:, :].bitcast(BF16)

            def wpiece(tag, W2d, c0, ncols, moe=False, rows=None):
                src = W2d.rearrange("(kc p) n -> p kc n", p=128)[:, :, c0:c0 + ncols]
                kc = W2d.shape[0] // 128
                return ws.next(tag, src, [P, kc, ncols], moe)

            S.op('pool', lambda e: e.iota(iota128, [[1, 128]], base=0, channel_multiplier=0,
                                          allow_small_or_imprecise_dtypes=True), (), ['iota128'])
            S.op('pool', lambda e: e.iota(identf, [[1, 128]], base=0, channel_multiplier=-1,
                                          allow_small_or_imprecise_dtypes=True), (), ['identf'])
            S.op('pool', lambda e: e.iota(tcol, [[128, 16]], base=0, channel_multiplier=1,
                                          allow_small_or_imprecise_dtypes=True), (), ['tcol'])
            S.op('pool', lambda e: e.memset(onesb, 1.0), (), ['onesb'])
            S.op('pool', lambda e: e.memset(onesf, 1.0), (), ['onesf'])
            S.op('pool', lambda e: e.memset(small[:, 0:1], -0.5), (), ['mhalf'])
            ts(identf, identf, 0.0, None, ALU.is_equal, None, ['identf'], ['identf'])
            cp(identb, identf, ['identf'], ['identb'])
            m0 = R1.mark()
            vst = R1.alloc([NV, 128], F32)
            dma(vst, vecs[:, :], (), ['vst'])
            tr(banks[0][:, 0:NV], vst, identf[0:NV, 0:NV], ['vst', 'identf'], ['ps0'])
            cp(Vc, banks[0][:, 0:NV], ['ps0'], ['Vc'])
            for l in range(2):
                dma(rw[:, l], router_w[l].rearrange("(kc p) n -> p kc n", p=128), (), ['rw'], q='pool')
            act(sc, vcol('c', 0, 8), AF.Silu, ['Vc'], ['sc'])
            cp(scB, sc.unsqueeze(2).to_broadcast([P, 8, 128]), ['sc'], ['scB'])
            act(esink, vcol('sink', 0, 4), AF.Exp, ['Vc'], ['esink'])
            act(nsp, vcol('lam', 0, 8), AF.Exp, ['Vc'], ['nsp'], scale=-1.0)
            act(nsp, nsp, AF.Ln, ['nsp'], ['nsp'], bias=1.0)
            ts(nsp, nsp, -8.0, None, ALU.mult, None, ['nsp'], ['nsp'])
            carry = vcol('flags', 0)
            ctxb = vcol('flags', 1)
            fS = vcol('flags', 2)
            fP = vcol('flags', 3)
            S.barrier()
            R1.release(m0)

            def adaln_cols(l):
                for i in range(2):
                    for half in range(2):
                        wv, wk = wpiece(f'modc{l}_{i}_{half}', mod_w[l], i * 1024 + half * 512, 512)
                        for oc in range(4):
                            col = i * 8 + half * 4 + oc
                            for kc in range(KC):
                                mm(banks[0][:, col:col + 1], wv[:, kc, oc * 128:(oc + 1) * 128], sc[:, kc:kc + 1],
                                   kc == 0, kc == KC - 1, [wk, 'sc'], ['ps0'])
                        ws.release(wk)
                tt(modc[:, l, 0:8], banks[0][:, 0:8], vcol(f'modb{l}_0', 0, 8), ALU.add, ['ps0', 'Vc'], ['modc'])
                tt(modc[:, l, 8:16], banks[0][:, 8:16], vcol(f'modb{l}_1', 0, 8), ALU.add, ['ps0', 'Vc'], ['modc'])
                ts(modc[:, l, 8:16], modc[:, l, 8:16], 1.0, None, ALU.add, None, ['modc'], ['modc'])

            def adaln_row(l, i, dst, key, plus1=False):
                dma(dst, rowsrc[RR[f'modb{l}_{i}']:RR[f'modb{l}_{i}'] + 1, :].partition_broadcast(P), (), [key])
                for half in range(2):
                    wv, wk = wpiece(f'modr{l}_{i}_{half}', mod_w[l], i * 1024 + half * 512, 512)
                    b = 1 + half
                    for kc in range(KC):
                        mm(banks[b][:, :], scB[:, kc, :], wv[:, kc, :], kc == 0, kc == KC - 1, [wk, 'scB'], [f'ps{b}'])
                    ws.release(wk)
                    tt(dst[:, half * 512:(half + 1) * 512], dst[:, half * 512:(half + 1) * 512], banks[b][:, :], ALU.add,
                       [f'ps{b}', key], [key])
                if plus1:
                    ts(dst, dst, 1.0, None, ALU.add, None, [key], [key])

            def load_row(dst, name, key):
                dma(dst, rowsrc[RR[name]:RR[name] + 1, :].partition_broadcast(P), (), [key])

            def make_hT(l, src_chunk, g4s=(0, 1, 2, 3)):
                for g4 in g4s:
                    for kc in range(KC):
                        b = (g4 * KC + kc) % 4
                        for j in range(4):
                            tc = g4 * 4 + j
                            xa, xk = src_chunk(tc)
                            tr(banks[b][:, j * 128:(j + 1) * 128], xa[:, kc * 128:(kc + 1) * 128], identf,
                               [xk, 'identf'], [f'ps{b}'])
                        act(hT[:, kc, g4 * 512:(g4 + 1) * 512], banks[b][:, :], AF.Identity, [f'ps{b}', 'modc'],
                            [('p2', kc, g4)], bias=modc[:, l, kc:kc + 1], scale=modc[:, l, 8 + kc:9 + kc])

            hT_keys = [('p2', kc, g) for kc in range(KC) for g in range(4)]

            def mixer_epilogue(l, mixT, mix_keys, Wd, tagp, rows):
                wv = []
                for dh in range(2):
                    wv.append(wpiece(f'{tagp}{dh}', Wd, dh * 512, 512))
                for dh in range(2):
                    for kc in range(KC):
                        tt(wv[dh][0][:, kc, :], wv[dh][0][:, kc, :], rows['g2'][:, dh * 512:(dh + 1) * 512], ALU.mult,
                           [wv[dh][1], 'row_g2'], [wv[dh][1]], eng='dve' if kc % 2 == 0 else 'pool')
                m = P4.mark()
                tmpg = [P4.alloc([P, 512], F32) for _ in range(2)]
                xnb_ = [P4.alloc([P, D], F32) for _ in range(2)]
                stt_ = P4.alloc([P, 32], F32)
                for tc in range(NT):
                    xn = xnb_[tc % 2]
                    xnk = f'xn{tc % 2}'
                    act(xres[:, tc, :], xres[:, tc, :], AF.Copy, [('xres', tc)], [('xres', tc)], scale=ALPHA)
                    for dh in range(2):
                        b = 4 + (tc * 2 + dh) % 4
                        for kc in range(KC):
                            mm(banks[b][:, :], mixT[kc][:, tc * 128:(tc + 1) * 128], wv[dh][0][:, kc, :], kc == 0, kc == KC - 1,
                               [wv[dh][1]] + mix_keys, [f'ps{b}'])
                        xs = xres[:, tc, dh * 512:(dh + 1) * 512]
                        tt(xs, xs, banks[b][:, :], ALU.add, [('xres', tc), f'ps{b}'], [('xres', tc)])
                    layer_norm_chunk(tc, stt_, xn, rows['lng'], rows['lnb'], 'row_lng', 'row_lnb')
                    tt(xn, xres[:, tc, :], rows['s4p'], ALU.mult, [('xres', tc), 'row_s4p'], [xnk], eng='pool')
                    tt(xm_tok[:, tc, :], xn, rows['s3'], ALU.add, [xnk, 'row_s3'], [('xm', tc)], eng='pool')
                ws.release(wv[0][1])
                ws.release(wv[1][1])
                P4.release(m)

            def layer_norm_chunk(tc, st2, xn, lng, lnb, kg, kb, tail_eng='dve'):
                xc_ = xres[:, tc, :]
                i2 = tc % 2
                st_ = st2[:, 16 * i2:16 * i2 + 16]
                xnk = f'xn{i2}'
                S.op('dve', lambda e: e.bn_stats(out=st_[:, 0:6], in_=xres[:, tc, 0:512]), [('xres', tc)], [f'lnst{i2}'])
                S.op('dve', lambda e: e.bn_stats(out=st_[:, 6:12], in_=xres[:, tc, 512:1024]), [('xres', tc)], [f'lnst{i2}'])
                S.op('dve', lambda e: e.bn_aggr(out=st_[:, 12:14], in_=st_[:, 0:12]), [f'lnst{i2}'], [f'lnmv{i2}'])
                ts(st_[:, 14:15], st_[:, 13:14], LN_EPS, None, ALU.add, None, [f'lnmv{i2}'], [f'lnr{i2}'], eng='pool')
                tt(st_[:, 14:15], st_[:, 14:15], small[:, 0:1], ALU.pow, [f'lnr{i2}', 'mhalf'], [f'lnr{i2}'], eng='pool')
                ts(st_[:, 15:16], st_[:, 12:13], st_[:, 14:15], -1.0, ALU.mult, ALU.mult, [f'lnmv{i2}', f'lnr{i2}'], [f'lnn{i2}'])
                act(xn, xc_, AF.Identity, [('xres', tc), f'lnr{i2}', f'lnn{i2}'], [xnk], bias=st_[:, 15:16], scale=st_[:, 14:15])
                tt(xn, xn, lng, ALU.mult, [xnk, kg], [xnk])
                tt(xc_, xn, lnb, ALU.add, [xnk, kb], [('xres', tc)], eng=tail_eng)

            def layer0_mixer():
                adaln_cols(0)
                m4 = P4.mark()
                attT = P4.alloc([P, 4, T], BF16)
                rnnT = P4.alloc([P, 4, T], BF16)
                rows = {}
                for nm in ('g2', 'lng', 'lnb', 's4p'):
                    rows[nm] = B2.alloc([P, D], F32)
                rows['s3'] = P4.alloc([P, D], F32)
                if cut == 1:
                    return
                m1 = R1.mark()
                xst = [R1.alloc([P, 4, D], F32) for _ in range(2)]
                for g4 in range(4):
                    dma(xst[g4 % 2], x_in[g4 * 512:(g4 + 1) * 512, :].rearrange("(j p) d -> p j d", p=128), (), [f'xst{g4 % 2}'])
                    for kc in range(KC):
                        b = (g4 * KC + kc) % 4
                        for j in range(4):
                            tr(banks[b][:, j * 128:(j + 1) * 128], xst[g4 % 2][:, j, kc * 128:(kc + 1) * 128], identf,
                               [f'xst{g4 % 2}', 'identf'], [f'ps{b}'])
                        if os.environ.get("MK_DBG") == "a":
                            continue
                        if os.environ.get("MK_DBG") == "c":
                            act(hT[:, kc, g4 * 512:(g4 + 1) * 512], banks[b][:, :], AF.Identity, [f'ps{b}', 'modc'], [('p2', kc, g4)],
                                bias=modc[:, 0, kc:kc + 1])
                            continue
                        if os.environ.get("MK_DBG") == "d":
                            act(hT[:, kc, g4 * 512:(g4 + 1) * 512], banks[b][:, :], AF.Identity, [f'ps{b}', 'modc'], [('p2', kc, g4)],
                                scale=modc[:, 0, 8 + kc:9 + kc])
                            continue
                        if os.environ.get("MK_DBG") == "b":
                            act(hT[:, kc, g4 * 512:(g4 + 1) * 512], banks[b][:, :], AF.Copy, [f'ps{b}', 'modc'], [('p2', kc, g4)])
                            continue
                        act(hT[:, kc, g4 * 512:(g4 + 1) * 512], banks[b][:, :], AF.Identity, [f'ps{b}', 'modc'],
                            [('p2', kc, g4)], bias=modc[:, 0, kc:kc + 1], scale=modc[:, 0, 8 + kc:9 + kc])
                S.barrier()
                R1.release(m1)
                adaln_row(0, 2, rows['g2'], 'row_g2')
                adaln_row(0, 3, rows['s3'], 'row_s3')
                adaln_row(0, 4, rows['s4p'], 'row_s4p', plus1=True)
                load_row(rows['lng'], 'ln_mix_g0', 'row_lng')
                load_row(rows['lnb'], 'ln_mix_b0', 'row_lnb')
                if cut == 2:
                    return
                qT = R1.alloc([P, 4, T], BF16)
                kT = R1.alloc([P, 2, T], BF16)
                vtok = R1.alloc([P, NT, 256], BF16)
                m2 = R1.mark()
                ropeC = R1.alloc([P, T], F32)
                ropeS = R1.alloc([P, T], F32)
                t1 = [R1.alloc([P, 512], F32) for _ in range(2)]
                t2 = [R1.alloc([P, 512], F32) for _ in range(2)]
                kvst = [R1.alloc([P, 512], F32) for _ in range(2)]
                dma(ropeC, rope_d[:, 0:T], (), ['ropeC'])
                dma(ropeS, rope_d[:, T:2 * T], (), ['ropeS'])
                wq, wqk = wpiece('w_q', w_in, 0, 512)
                wqs, wqsk = wpiece('w_qsw', w_in, 2048, 512)
                it = 0
                for h in range(4):
                    for tg in range(4):
                        for kc in range(KC):
                            mm(banks[0][:, :], wq[:, kc, h * 128:(h + 1) * 128], hT[:, kc, tg * 512:(tg + 1) * 512],
                               kc == 0, kc == KC - 1, [wqk] + hT_keys, ['ps0'])
                        for kc in range(KC):
                            mm(banks[1][:, :], wqs[:, kc, h * 128:(h + 1) * 128], hT[:, kc, tg * 512:(tg + 1) * 512],
                               kc == 0, kc == KC - 1, [wqsk] + hT_keys, ['ps1'])
                        i2 = it % 2
                        it += 1
                        tt(t1[i2], banks[0][:, :], ropeC[:, tg * 512:(tg + 1) * 512], ALU.mult, ['ps0', 'ropeC'], [f't1{i2}'])
                        tt(t2[i2], banks[1][:, :], ropeS[:, tg * 512:(tg + 1) * 512], ALU.mult, ['ps1', 'ropeS'], [f't2{i2}'])
                        tt(qT[:, h, tg * 512:(tg + 1) * 512], t1[i2], t2[i2], ALU.add, [f't1{i2}', f't2{i2}'], ['qT'], eng=ROPE_ENG)
                ws.release(wqk)
                ws.release(wqsk)
                if cut == 21:
                    return
                wkv, wkvk = wpiece('w_kv', w_in, 512, 512)
                wks, wksk = wpiece('w_ksw', w_in, 2560, 256)
                for h in range(2):
                    for tg in range(4):
                        for kc in range(KC):
                            mm(banks[0][:, :], wkv[:, kc, h * 128:(h + 1) * 128], hT[:, kc, tg * 512:(tg + 1) * 512],
                               kc == 0, kc == KC - 1, [wkvk] + hT_keys, ['ps0'])
                        for kc in range(KC):
                            mm(banks[1][:, :], wks[:, kc, h * 128:(h + 1) * 128], hT[:, kc, tg * 512:(tg + 1) * 512],
                               kc == 0, kc == KC - 1, [wksk] + hT_keys, ['ps1'])
                        i2 = it % 2
                        it += 1
                        tt(t1[i2], banks[0][:, :], ropeC[:, tg * 512:(tg + 1) * 512], ALU.mult, ['ps0', 'ropeC'], [f't1{i2}'])
                        tt(t2[i2], banks[1][:, :], ropeS[:, tg * 512:(tg + 1) * 512], ALU.mult, ['ps1', 'ropeS'], [f't2{i2}'])
                        tt(kT[:, h, tg * 512:(tg + 1) * 512], t1[i2], t2[i2], ALU.add, [f't1{i2}', f't2{i2}'], ['kT'], eng=ROPE_ENG)
                ws.release(wksk)
                if cut == 22:
                    return
                for tc in range(NT):
                    b = 2 + tc % 2
                    for kc in range(KC):
                        mm(banks[b][:, :], hT[:, kc, tc * 128:(tc + 1) * 128], wkv[:, kc, :], kc == 0, kc == KC - 1,
                           [wkvk] + hT_keys, [f'ps{b}'])
                    i2 = tc % 2
                    act(kvst[i2], banks[b][:, :], AF.Copy, [f'ps{b}'], [f'kvst{i2}'])
                    cp(vtok[:, tc, :], banks[b][:, 256:512], [f'ps{b}'], ['vtok'])
                    if os.environ.get("MK_DBG2") != "g":
                        dma(kv_d[tc * 128:(tc + 1) * 128, :], kvst[i2], [f'kvst{i2}'], ['kv_d'])
                ws.release(wkvk)
                S.barrier()
                R1.release(m2)
                if cut == 3:
                    return
                maskLR = R1.alloc([P, 16, 2, 128], BF16)
                kctx = R1.alloc([P, 2, 256], BF16)
                vctx = R1.alloc([P, 2, 256], BF16)
                sm = [R1.alloc([P, 512], F32) for _ in range(2)]
                pT = [R1.alloc([P, 5, 256], BF16) for _ in range(2)]
                rden = [R1.alloc([P, 256], F32) for _ in range(2)]
                dma(maskLR, mask_d[:, :].rearrange("p (n s q) -> p n s q", s=2, q=128), (), ['maskLR'], q='pool')
                dma(kctx, kctx_d[:, :].rearrange("p (k t) -> p k t", t=256), (), ['kctx'], q='pool')
                dma(vctx, vctx_d[:, :].rearrange("p (b f) -> p b f", f=256), (), ['vctx'], q='pool')
                def att_A(itn):
                    n, kv = itn // 2, itn % 2
                    nl = max(n - 1, 0)
                    nr = min(n + 1, NT - 1)
                    i2 = itn % 2
                    bs0, bs1, bs2, bo = (0, 1, 2, 3) if i2 == 0 else (4, 5, 6, 7)
                    qa = qT[:, 2 * kv:2 * kv + 2, n * 128:(n + 1) * 128]
                    keyb = [kctx[:, kv, 0:128], kctx[:, kv, 128:256], kT[:, kv, nl * 128:(nl + 1) * 128],
                            kT[:, kv, n * 128:(n + 1) * 128], kT[:, kv, nr * 128:(nr + 1) * 128]]
                    outs = [banks[bs0][:, 0:256], banks[bs0][:, 256:512], banks[bs1][:, 0:256], banks[bs1][:, 256:512],
                            banks[bs2][:, 0:256]]
                    okeys = [f'ps{bs0}', f'ps{bs0}', f'ps{bs1}', f'ps{bs1}', f'ps{bs2}']
                    for kb in range(5):
                        mm(outs[kb], keyb[kb], qa, True, True, ['qT', 'kT', 'kctx'], [okeys[kb]])
                    p_ = pT[i2]
                    pk = f'pT{i2}'
                    act(p_[:, 0:2, :], banks[bs0][:, :].rearrange("p (a b) -> p a b", b=256), AF.Exp, [f'ps{bs0}', 'Vc'], [pk],
                        bias=ctxb, scale=ATTN_SCALE)
                    s_ = sm[i2]
                    mL = maskLR[:, n, 0, :].unsqueeze(1).to_broadcast([P, 2, 128])
                    mR = maskLR[:, n, 1, :].unsqueeze(1).to_broadcast([P, 2, 128])
                    stt(s_[:, 0:256].rearrange("p (a b) -> p a b", b=128), banks[bs1][:, 0:256].rearrange("p (a b) -> p a b", b=128),
                        ATTN_SCALE, mL, ALU.mult, ALU.add, [f'ps{bs1}', 'maskLR'], [f'sm{i2}'])
                    stt(s_[:, 256:512].rearrange("p (a b) -> p a b", b=128), banks[bs2][:, 0:256].rearrange("p (a b) -> p a b", b=128),
                        ATTN_SCALE, mR, ALU.mult, ALU.add, [f'ps{bs2}', 'maskLR'], [f'sm{i2}'])
                    act(p_[:, 3, :], banks[bs1][:, 256:512], AF.Exp, [f'ps{bs1}'], [pk], scale=ATTN_SCALE)
                    act(p_[:, 2, :], s_[:, 0:256], AF.Exp, [f'sm{i2}'], [pk])
                    act(p_[:, 4, :], s_[:, 256:512], AF.Exp, [f'sm{i2}'], [pk])

                def att_B(itn):
                    n, kv = itn // 2, itn % 2
                    nl = max(n - 1, 0)
                    nr = min(n + 1, NT - 1)
                    i2 = itn % 2
                    bo = 3 if i2 == 0 else 7
                    p_ = pT[i2]
                    pk = f'pT{i2}'
                    vb = [vctx[:, 0, kv * 128:(kv + 1) * 128], vctx[:, 1, kv * 128:(kv + 1) * 128],
                          vtok[:, nl, kv * 128:(kv + 1) * 128], vtok[:, n, kv * 128:(kv + 1) * 128],
                          vtok[:, nr, kv * 128:(kv + 1) * 128]]
                    for kb in range(5):
                        mm(banks[bo][:, 0:256], vb[kb], p_[:, kb, :], kb == 0, kb == 4, [pk, 'vtok', 'vctx'], [f'ps{bo}'])
                    for kb in range(5):
                        mm(banks[bo][:, 256:512], onesb, p_[:, kb, :], kb == 0, kb == 4, [pk, 'onesb'], [f'ps{bo}'])
                    rd = rden[i2]
                    for g in range(2):
                        act(rd[:, g * 128:(g + 1) * 128], banks[bo][:, 256 + g * 128:256 + (g + 1) * 128], AF.Ln,
                            [f'ps{bo}', 'esink'], [f'rden{i2}'], bias=esink[:, 2 * kv + g:2 * kv + g + 1])
                    act(rd, rd, AF.Exp, [f'rden{i2}'], [f'rden{i2}'], scale=-1.0)
                    tt(attT[:, 2 * kv:2 * kv + 2, n * 128:(n + 1) * 128], banks[bo][:, 0:256].rearrange("p (a b) -> p a b", b=128),
                       rd.rearrange("p (a b) -> p a b", b=128), ALU.mult, [f'ps{bo}', f'rden{i2}'], ['attT'])

                NIT = NT * 2
                att_A(0)
                for itn in range(NIT):
                    if itn + 1 < NIT:
                        att_A(itn + 1)
                    att_B(itn)
                S.barrier()
                R1.release(0)
                if cut == 4:
                    return
                mrg = P4.mark()
                bdm = P4.alloc([P, 16, 128], BF16)
                dma(bdm, bd_d.rearrange("m p q -> p m q"), (), ['bdm'], q='pool')
                xrp = R1.alloc([P, 8, 259], F32)
                xc = R1.alloc([P, 8, 256], F32)
                xcb = R1.alloc([P, T], BF16)
                a_t = R1.alloc([P, T], F32)
                i_t = R1.alloc([P, T], F32)
                a_t2 = P4.alloc([P, T], F32)
                i_t2 = P4.alloc([P, T], F32)
                tmp = R1.alloc([P, T], F32)
                hf = R1.alloc([P, T], F32)
                hb = R1.alloc([P, T], F32)
                stT = P4.alloc([P, 64], F32)
                stO = P4.alloc([64, 128], F32)
                wxr, wxrk = wpiece('w_xr', w_in, 1024, 512)
                wxg, wxgk = wpiece('w_xg', w_in, 1536, 512)
                xc2 = xc.rearrange("p a b -> p (a b)")
                def rg_front(c):
                    S.op('pool', lambda e: e.memset(xrp, 0.0), (), ['xrp'])
                    for tg in range(4):
                        b = tg % 2
                        for kc in range(KC):
                            mm(banks[b][:, :], wxr[:, kc, c * 128:(c + 1) * 128], hT[:, kc, tg * 512:(tg + 1) * 512],
                               kc == 0, kc == KC - 1, [wxrk] + hT_keys, [f'ps{b}'])
                        act(xrp[:, 2 * tg:2 * tg + 2, 2:258], banks[b][:, :].rearrange("p (a b) -> p a b", b=256), AF.Copy,
                            [f'ps{b}'], ['xrp'])
                    ts(xrp[:, 1:8, 0:2], xrp[:, 0:7, 256:258], carry, None, ALU.mult, None, ['xrp', 'Vc'], ['xrp'])
                    ts(xrp[:, 0:7, 258:259], xrp[:, 1:8, 2:3], carry, None, ALU.mult, None, ['xrp', 'Vc'], ['xrp'])
                    ts(xc, xrp[:, :, 0:256], vcol('convw', 0 * 4 + c), vcol('convb', c), ALU.mult, ALU.add, ['xrp', 'Vc'], ['xc'], eng='pool')
                    for i in range(1, 4):
                        stt(xc, xrp[:, :, i:i + 256], vcol('convw', i * 4 + c), xc, ALU.mult, ALU.add, ['xrp', 'xc', 'Vc'], ['xc'])
                    act(xcb, xc2, AF.Copy, ['xc'], ['xcb'])

                def rg_mid(c):
                    A_ = [a_t, a_t2]
                    I_ = [i_t, i_t2]
                    H_ = [hf, hb]
                    AK = ['a_t', 'a_t2']
                    IK = ['i_t', 'i_t2']
                    HK = ['hf', 'hb']
                    for dr in range(2):
                        for tg in range(4):
                            b = 2 + tg % 2
                            mm(banks[b][:, :], bdm[:, (dr * 2 + 0) * 4 + c, :], xcb[:, tg * 512:(tg + 1) * 512], True, True,
                               ['bdm', 'xcb'], [f'ps{b}'])
                            act(A_[dr][:, tg * 512:(tg + 1) * 512], banks[b][:, :], AF.Sigmoid, [f'ps{b}', 'Vc'], [AK[dr]],
                                bias=vcol('ba', dr * 4 + c))
                            b2 = 4 + tg % 2
                            mm(banks[b2][:, :], bdm[:, (dr * 2 + 1) * 4 + c, :], xcb[:, tg * 512:(tg + 1) * 512], True, True,
                               ['bdm', 'xcb'], [f'ps{b2}'])
                            act(I_[dr][:, tg * 512:(tg + 1) * 512], banks[b2][:, :], AF.Sigmoid, [f'ps{b2}', 'Vc'], [IK[dr]],
                                bias=vcol('bx', dr * 4 + c))
                    for dr in range(2):
                        act(A_[dr], A_[dr], AF.Exp, [AK[dr], 'nsp'], [AK[dr]], scale=nsp[:, dr * 4 + c:dr * 4 + c + 1])
                    for dr in range(2):
                        tt(I_[dr], I_[dr], xc2, ALU.mult, [IK[dr], 'xc'], [IK[dr]], eng='pool')

                def rg_late(c):
                    A_ = [a_t, a_t2]
                    I_ = [i_t, i_t2]
                    H_ = [hf, hb]
                    AK = ['a_t', 'a_t2']
                    IK = ['i_t', 'i_t2']
                    HK = ['hf', 'hb']
                    for dr in range(2):
                        tt(H_[dr], A_[dr], A_[dr], ALU.mult, [AK[dr]], [HK[dr]])
                    for dr in range(2):
                        act(H_[dr], H_[dr], AF.Sqrt, [HK[dr]], [HK[dr]], bias=1.0, scale=-1.0)
                    for dr in range(2):
                        tt(I_[dr], I_[dr], H_[dr], ALU.mult, [IK[dr], HK[dr]], [IK[dr]])
                        h0c = vcol('h0', dr * 4 + c)
                        if dr == 0:
                            av = A_[dr].rearrange("p (s t) -> p s t", t=256)[:, 1:8, 0:1]
                            ts(av, av, carry, None, ALU.mult, None, [AK[dr], 'Vc'], [AK[dr]])
                            S.op('dve', lambda e, h0c=h0c: e.tensor_tensor_scan(out=hf, data0=a_t, data1=i_t, initial=h0c,
                                                                                op0=ALU.mult, op1=ALU.add),
                                 [AK[dr], IK[dr], 'Vc'], [HK[dr]])
                        else:
                            av = A_[dr].rearrange("p (s t) -> p s t", t=256)[:, 0:7, 255:256]
                            ts(av, av, carry, None, ALU.mult, None, [AK[dr], 'Vc'], [AK[dr]])
                            S.op('dve', lambda e, h0c=h0c: e.tensor_tensor_scan(out=hb[:, ::-1], data0=a_t2[:, ::-1],
                                                                                data1=i_t2[:, ::-1], initial=h0c,
                                                                                op0=ALU.mult, op1=ALU.add),
                                 [AK[dr], IK[dr], 'Vc'], [HK[dr]])
                    cp(stT[:, (0 * 4 + c) * 8:(0 * 4 + c) * 8 + 8], hf.rearrange("p (s t) -> p s t", t=256)[:, :, 255], ['hf'], ['stT'])
                    cp(stT[:, (1 * 4 + c) * 8:(1 * 4 + c) * 8 + 8], hb.rearrange("p (s t) -> p s t", t=256)[:, :, 0], ['hb'], ['stT'])
                    tt(hf, hf, hb, ALU.add, ['hf', 'hb'], ['hf'])
                    for tg in range(4):
                        b = 6 + tg % 2
                        for kc in range(KC):
                            mm(banks[b][:, :], wxg[:, kc, c * 128:(c + 1) * 128], hT[:, kc, tg * 512:(tg + 1) * 512],
                               kc == 0, kc == KC - 1, [wxgk] + hT_keys, [f'ps{b}'])
                        act(tmp[:, tg * 512:(tg + 1) * 512], banks[b][:, :], AF.Gelu_apprx_tanh, [f'ps{b}'], ['tmp'])
                    tt(rnnT[:, c, :], hf, tmp, ALU.mult, ['hf', 'tmp'], ['rnnT'])

                rg_front(0)
                for c in range(4):
                    rg_mid(c)
                    if c + 1 < 4:
                        rg_front(c + 1)
                    rg_late(c)
                ws.release(wxrk)
                ws.release(wxgk)
                tr(banks[0][0:64, 0:128], stT, identf, ['stT', 'identf'], ['ps0'])
                cp(stO, banks[0][0:64, 0:128], ['ps0'], ['stO'])
                for dr in range(2):
                    for c in range(4):
                        r0 = (dr * 4 + c) * 8
                        dma(st_d[:, dr * 512 + c * 128:dr * 512 + (c + 1) * 128], stO[r0:r0 + 8, :], ['stO'], ['st_d'])
                S.barrier()
                R1.release(0)
                P4.release(mrg)
                if cut == 5:
                    return
                for tc in range(NT):
                    dma(xres[:, tc, :], x_in[tc * 128:(tc + 1) * 128, :], (), [('xres', tc)])
                mixT = [attT[:, h, :] for h in range(4)] + [rnnT[:, c, :] for c in range(4)]
                mixer_epilogue(0, mixT, ['attT', 'rnnT'], ab_out_w, 'w_out0_', rows)
                S.barrier()
                P4.release(m4)

            def moe(l, last):
                S.barrier()
                ws.set_wide(True)
                m4 = P4.mark()
                g5 = P4.alloc([P, D], F32)
                lng = P4.alloc([P, D], F32)
                lnb = P4.alloc([P, D], F32)
                adaln_row(l, 5, g5, 'row_g5')
                load_row(lng, f'ln_ffn_g{l}', 'row_lng')
                load_row(lnb, f'ln_ffn_b{l}', 'row_lnb')
                idxT = P4.alloc([P, 2, NE], F32)
                gT = P4.alloc([P, 2, NE], F32)
                idxf = P4.alloc([NE, 256], F32)
                mr = P4.mark()
                xmT = [P4.alloc([P, 8, 128], BF16) for _ in range(2)]
                aff = P4.alloc([NE, T], F32)
                wS = P4.alloc([NE, T], F32)
                idxS = P4.alloc([NE, 256], F32)
                gS = P4.alloc([NE, 256], F32)
                idxP = P4.alloc([NE, 256], F32)
                gP = P4.alloc([NE, 256], F32)
                gf = P4.alloc([NE, 256], F32)
                offr = P4.alloc([NE, 256], F32)
                mi = P4.alloc([NE, 8], U32)
                def rt_A(tc):
                    i2 = tc % 2
                    bt = 4 + i2
                    for kc in range(KC):
                        tr(bank_bf(bt)[:, kc * 128:(kc + 1) * 128], xm_tok[:, tc, kc * 128:(kc + 1) * 128], identb,
                           [('xm', tc), 'identb'], [f'ps{bt}'])
                    if tc % 2 == 0:
                        cp(xmT[i2], bank_bf(bt)[:, 0:1024].rearrange("p (a b) -> p a b", b=128), [f'ps{bt}'], [f'xmT{i2}'])
                    else:
                        act(xmT[i2], bank_bf(bt)[:, 0:1024].rearrange("p (a b) -> p a b", b=128), AF.Copy, [f'ps{bt}'], [f'xmT{i2}'])

                def rt_B(tc):
                    i2 = tc % 2
                    bl = tc // 4
                    for kc in range(KC):
                        mm(banks[bl][0:NE, (tc % 4) * 128:(tc % 4 + 1) * 128], rw[:, l, kc, :], xmT[i2][:, kc, :], kc == 0, kc == KC - 1,
                           [f'xmT{i2}', 'rw'], [f'ps{bl}'])

                rt_A(0)
                for tc in range(NT):
                    if tc + 1 < NT:
                        rt_A(tc + 1)
                    rt_B(tc)
                for bl in range(4):
                    act(aff[:, bl * 512:(bl + 1) * 512], banks[bl][0:NE, :], AF.Exp, [f'ps{bl}'], ['aff'])
                for bl in range(4):
                    mm(banks[bl][0:NE, :], onesf[0:NE, 0:NE], aff[:, bl * 512:(bl + 1) * 512], True, True, ['aff', 'onesf'], [f'ps{bl}'])
                    S.op('dve', lambda e, bl=bl: e.reciprocal(out=wS[:, bl * 512:(bl + 1) * 512], in_=banks[bl][0:NE, :]), [f'ps{bl}'], ['wS'])
                tt(aff, aff, wS, ALU.mult, ['aff', 'wS'], ['aff'])
                S.op('pool', lambda e: e.iota(offr, [[256, 8], [0, 32]], base=0, channel_multiplier=0,
                                              allow_small_or_imprecise_dtypes=True), (), ['offr'])
                S.barrier()
                w32 = wS[0:32, 0:1024] if False else P4.alloc([32, 1024], F32)
                g32 = P4.alloc([32, 256], F32)
                i32f = P4.alloc([32, 256], F32)
                g1 = P4.alloc([NE, 256], F32)
                i1 = P4.alloc([NE, 256], F32)
                m0 = P4.alloc([NE, 256], F32)
                mi32 = P4.alloc([32, 256], U32)
                offc = P4.alloc([32, 1], F32)
                w128 = P4.alloc([P, 256], F32)
                g128 = P4.alloc([P, 32], F32)
                i128 = P4.alloc([P, 32], F32)
                mi128 = P4.alloc([P, 32], U32)
                for s_ in range(8):
                    dma(w128[16 * s_:16 * s_ + 16, :], aff[:, 256 * s_:256 * s_ + 256], ['aff'], ['w128'])
                cp(w32[0:NE, :], aff[:, 0:1024], ['aff'], ['w32'])
                dma(w32[NE:32, :], aff[:, 1024:2048], ['aff'], ['w32'])
                ts(offc, tcol[0:32, 0:1], 16.0, 1024.0, ALU.is_ge, ALU.mult, ['tcol'], ['offc'])
                for r in range(32):
                    sl = slice(8 * r, 8 * r + 8)
                    S.op('dve', lambda e, sl=sl: e.max(out=g32[:, sl], in_=w32), ['w32'], ['g32'])
                    S.op('dve', lambda e, sl=sl: e.max_index(out=mi32[:, sl], in_max=g32[:, sl], in_values=w32), ['w32', 'g32'], ['mi32'])
                    S.op('dve', lambda e, sl=sl: e.match_replace(out=w32, in_to_replace=g32[:, sl], in_values=w32, imm_value=0.0),
                         ['w32', 'g32'], ['w32'])
                cp(i32f, mi32, ['mi32'], ['i32f'])
                ts(i32f, i32f, offc, None, ALU.add, None, ['i32f', 'offc'], ['i32f'])
                dma(g1, g32[NE:32, :], ['g32'], ['g1'])
                dma(i1, i32f[NE:32, :], ['i32f'], ['i1'])
                g0 = g32[0:NE, :]
                i0 = i32f[0:NE, :]
                tt(m0, g0, g1[:, ::-1], ALU.is_ge, ['g32', 'g1'], ['m0'])
                tt(gS, g0, g1[:, ::-1], ALU.max, ['g32', 'g1'], ['gS'])
                tt(idxS, i0, i1[:, ::-1], ALU.subtract, ['i32f', 'i1'], ['idxS'])
                tt(idxS, idxS, m0, ALU.mult, ['idxS', 'm0'], ['idxS'])
                tt(idxS, idxS, i1[:, ::-1], ALU.add, ['idxS', 'i1'], ['idxS'])
                for r in range(4):
                    sl = slice(8 * r, 8 * r + 8)
                    S.op('dve', lambda e, sl=sl: e.max(out=g128[:, sl], in_=w128), ['w128'], ['g128'])
                    S.op('dve', lambda e, sl=sl: e.max_index(out=mi128[:, sl], in_max=g128[:, sl], in_values=w128), ['w128', 'g128'], ['mi128'])
                    S.op('dve', lambda e, sl=sl: e.match_replace(out=w128, in_to_replace=g128[:, sl], in_values=w128, imm_value=0.0),
                         ['w128', 'g128'], ['w128'])
                cp(i128, mi128, ['mi128'], ['i128'])
                for s_ in range(8):
                    dma(gP[:, 32 * s_:32 * s_ + 32], g128[16 * s_:16 * s_ + 16, :], ['g128'], ['gP'])
                    dma(idxP[:, 32 * s_:32 * s_ + 32], i128[16 * s_:16 * s_ + 16, :], ['i128'], ['idxP'])
                tt(idxP, idxP, offr, ALU.add, ['idxP', 'offr'], ['idxP'])
                ts(idxf, idxS, fS[0:NE], None, ALU.mult, None, ['idxS', 'Vc'], ['idxf'])
                stt(idxf, idxP, fP[0:NE], idxf, ALU.mult, ALU.add, ['idxP', 'idxf', 'Vc'], ['idxf'])
                ts(gf, gS, fS[0:NE], None, ALU.mult, None, ['gS', 'Vc'], ['gf'])
                stt(gf, gP, fP[0:NE], gf, ALU.mult, ALU.add, ['gP', 'gf', 'Vc'], ['gf'])
                for jc in range(2):
                    tr(banks[0][:, jc * 16:(jc + 1) * 16], idxf[:, jc * 128:(jc + 1) * 128], identf[0:NE, 0:NE], ['idxf', 'identf'], ['ps0'])
                    tr(banks[0][:, 32 + jc * 16:32 + (jc + 1) * 16], gf[:, jc * 128:(jc + 1) * 128], identf[0:NE, 0:NE], ['gf', 'identf'], ['ps0'])
                cp(idxT, banks[0][:, 0:32].rearrange("p (a b) -> p a b", b=16), ['ps0'], ['idxT'])
                cp(gT, banks[0][:, 32:64].rearrange("p (a b) -> p a b", b=16), ['ps0'], ['gT'])
                S.barrier()
                P4.release(mr)
                me_ = P4.mark()
                yg = [P4.alloc([P, GRP, 2, D], BF16) for _ in range(2)]
                sel = P4.alloc([P, NT, 256], BF16)
                xgT = [P4.alloc([P, 8, 256], BF16) for _ in range(2)]
                hid = [P4.alloc([P, 4, 256], BF16) for _ in range(2)]
                s1 = [P4.alloc([P, 256], F32) for _ in range(2)]
                selT = [P4.alloc([P, GRP * 2, 128], BF16) for _ in range(2)]
                rhe = P4.alloc([NE, 256], F32)
                YB = [0, 1, 2, 3]
                sc_it = [0]

                def selT_build(grp, tc, i2):
                    sT = selT[i2]
                    for q in range(GRP * 2):
                        ee = grp * GRP + q // 2
                        jc = q % 2
                        ts(sT[:, q, :], iota128, float(128 * tc), idxT[:, jc, ee:ee + 1], ALU.add, ALU.is_equal,
                           ['iota128', 'idxT'], [f'selT{i2}'])

                def scatter_mm(grp, tc, i2):
                    ygb = yg[grp % 2]
                    ygk = f'yg{grp % 2}'
                    sT = selT[i2]
                    sk = f'selT{i2}'
                    for dh in range(2):
                        b = 6 + dh
                        rk = [f'ps{b}']
                        for q in range(GRP * 2):
                            mm(banks[b][:, :], sT[:, q, :], ygb[:, q // 2, q % 2, dh * 512:(dh + 1) * 512], q == 0, q == GRP * 2 - 1,
                               [sk, ygk], rk)
                        xs = xres[:, tc, dh * 512:(dh + 1) * 512]
                        tt(xs, xs, banks[b][:, :], ALU.add, rk + [('xres', tc)], [('xres', tc)])

                def sel_build(e_):
                    ts(rhe, idxf, identf[0:NE, e_:e_ + 1], None, ALU.mult, None, ['idxf', 'identf'], ['rhe'])
                    mm(banks[7][:, 0:256], onesf[0:NE, :], rhe, True, True, ['rhe', 'onesf'], ['ps7'])
                    for tc in range(NT):
                        ts(sel[:, tc, :], banks[7][:, 0:256], tcol[:, tc:tc + 1], None, ALU.is_equal, None, ['ps7', 'tcol'], [('sel', tc)])

                def gather(e_):
                    xg_ = xgT[e_ % 2]
                    xgk = f'xgT{e_ % 2}'
                    for dc in range(KC):
                        b = 6 + dc % 2
                        for tc in range(NT):
                            mm(banks[b][:, 0:256], xm_tok[:, tc, dc * 128:(dc + 1) * 128], sel[:, tc, :], tc == 0, tc == NT - 1,
                               [('xm', tc), ('sel', tc)], [f'ps{b}'])
                        act(xg_[:, dc, :], banks[b][:, 0:256], AF.Copy, [f'ps{b}'], [xgk])

                for tc in range(NT):
                    act(xres[:, tc, :], xres[:, tc, :], AF.Copy, [('xres', tc)], [('xres', tc)], scale=ALPHA)
                NSLOT = GRP * 4
                TCS = NT // NSLOT
                sel_build(0)
                gather(0)
                for e_ in range(NE):
                    el = e_ % GRP
                    grp = e_ // GRP
                    ygb = yg[grp % 2]
                    ygk = f'yg{grp % 2}'
                    xg_ = xgT[e_ % 2]
                    xgk = f'xgT{e_ % 2}'
                    for fg in range(4):
                        w1v, w1k = wpiece(f'w1_{l}_{e_}_{fg}', moe_w1[l, e_], fg * 512, 512, moe=True)
                        w3v, w3k = wpiece(f'w3_{l}_{e_}_{fg}', moe_w3[l, e_], fg * 512, 512, moe=True)
                        src2 = moe_w2[l, e_, fg * 512:(fg + 1) * 512, :].rearrange("(fc p) d -> p fc d", p=128)
                        w2v, w2k = ws.next(f'w2_{l}_{e_}_{fg}', src2, [P, 4, D], True)
                        hd = hid[fg % 2]
                        hk = f'hid{fg % 2}'
                        slot = el * 4 + fg
                        if grp >= 1:
                            for i2 in range(TCS):
                                selT_build(grp - 1, slot * TCS + i2, i2)
                        for fc in range(4):
                            b = 4 + fc % 2
                            for dc in range(KC):
                                mm(banks[b][:, 0:256], w1v[:, dc, fc * 128:(fc + 1) * 128], xg_[:, dc, :], dc == 0, dc == KC - 1,
                                   [w1k, xgk], [f'ps{b}'])
                            for dc in range(KC):
                                mm(banks[b][:, 256:512], w3v[:, dc, fc * 128:(fc + 1) * 128], xg_[:, dc, :], dc == 0, dc == KC - 1,
                                   [w3k, xgk], [f'ps{b}'])
                            s_ = s1[fc % 2]
                            act(s_, banks[b][:, 0:256], AF.Silu, [f'ps{b}'], [f's1{fc % 2}'])
                            tt(hd[:, fc, :], s_, banks[b][:, 256:512], ALU.mult, [f'ps{b}', f's1{fc % 2}'], [hk])
                        for jc in range(2):
                            if grp >= 1:
                                scatter_mm(grp - 1, slot * TCS + jc, jc)
                            for dh in range(2):
                                yb = YB[jc * 2 + dh]
                                for fc in range(4):
                                    mm(banks[yb][:, :], hd[:, fc, jc * 128:(jc + 1) * 128], w2v[:, fc, dh * 512:(dh + 1) * 512],
                                       fg == 0 and fc == 0, fg == 3 and fc == 3, [hk, w2k], [f'ps{yb}'])
                        ws.release(w1k)
                        ws.release(w3k)
                        ws.release(w2k)
                        if e_ + 1 < NE:
                            if fg == 1:
                                sel_build(e_ + 1)
                            if fg == 2:
                                gather(e_ + 1)
                    for jc in range(2):
                        for dh in range(2):
                            yb = YB[jc * 2 + dh]
                            stt(ygb[:, el, jc, dh * 512:(dh + 1) * 512], banks[yb][:, :], gT[:, jc, e_:e_ + 1], g5[:, dh * 512:(dh + 1) * 512],
                                ALU.mult, ALU.mult, [f'ps{yb}', 'gT', 'row_g5'], [ygk])
                S.barrier()
                stt_ = P4.alloc([P, 32], F32)
                selflat = sel.rearrange("p a b -> p (a b)").bitcast(F32)
                xn_ = [selflat[:, 0:D], selflat[:, D:2 * D]]
                if not last:
                    adaln_cols(l + 1)
                selT_build(NE // GRP - 1, 0, 0)
                for tc in range(NT):
                    if tc + 1 < NT:
                        selT_build(NE // GRP - 1, tc + 1, (tc + 1) % 2)
                    scatter_mm(NE // GRP - 1, tc, tc % 2)
                    layer_norm_chunk(tc, stt_, xn_[tc % 2], lng, lnb, 'row_lng', 'row_lnb', tail_eng='pool')
                    if not last and tc % 4 == 3:
                        make_hT(l + 1, lambda t: (xres[:, t, :], ('xres', t)), g4s=(tc // 4,))
                    if last:
                        dma(y_d[tc * 128:(tc + 1) * 128, :], xres[:, tc, :], [('xres', tc)], ['y_d'])
                S.barrier()
                P4.release(m4)
                ws.set_wide(False)

            def layer1_mixer():
                S.barrier()
                m4 = P4.mark()
                uT = P4.alloc([P, 8, T], BF16)
                rowsA = [B2.alloc([P, D], F32) for _ in range(4)]
                mg = P4.mark()
                spw = P4.alloc([P, 8, 128], BF16)
                dma(spw, spwT_d[:, :].rearrange("p (g q) -> p g q", q=128), (), ['spw'], q='pool')
                for half in range(2):
                    wv, wk = wpiece(f'w_u{half}', sgu_in_w, half * 512, 512)
                    for oc in range(4):
                        for tg in range(4):
                            b = (oc * 4 + tg) % 4
                            for kc in range(KC):
                                mm(banks[b][:, :], wv[:, kc, oc * 128:(oc + 1) * 128], hT[:, kc, tg * 512:(tg + 1) * 512],
                                   kc == 0, kc == KC - 1, [wk] + hT_keys, [f'ps{b}'])
                            act(uT[:, half * 4 + oc, tg * 512:(tg + 1) * 512], banks[b][:, :], AF.Gelu_apprx_tanh, [f'ps{b}', 'Vc'], ['uT'],
                                bias=vcol('sgub_u', half * 4 + oc))
                    ws.release(wk)
                vb, lg, lb, spb = rowsA[0], rowsA[1], rowsA[2], rowsA[3]
                load_row(vb, 'sgub_v', 'row_vb')
                load_row(lg, 'sgu_ln_g', 'row_lg')
                load_row(lb, 'sgu_ln_b', 'row_lb')
                load_row(spb, 'spb', 'row_spb')
                wvv = [wpiece(f'w_v{half}', sgu_in_w, 1024 + half * 512, 512) for half in range(2)]
                vt = [P4.alloc([P, D], F32) for _ in range(2)]
                vnb = [P4.alloc([P, D], BF16) for _ in range(2)]
                st_ = P4.alloc([P, 32], F32)
                mxs = P4.alloc([P, D], F32)
                mxs2 = [mxs, P4.alloc([P, D], F32)]

                def sgu_A(tc):
                    i2 = tc % 2
                    v_ = vt[i2]
                    vk = f'vt{i2}'
                    for half in range(2):
                        b = half
                        for kc in range(KC):
                            mm(banks[b][:, :], hT[:, kc, tc * 128:(tc + 1) * 128], wvv[half][0][:, kc, :], kc == 0, kc == KC - 1,
                               [wvv[half][1]] + hT_keys, [f'ps{b}'])
                        tt(v_[:, half * 512:(half + 1) * 512], banks[b][:, :], vb[:, half * 512:(half + 1) * 512], ALU.add,
                           [f'ps{b}', 'row_vb'], [vk])
                    act(v_, v_, AF.Gelu_apprx_tanh, [vk], [vk])
                    sx = st_[:, 16 * i2:16 * i2 + 16]
                    S.op('dve', lambda e, v_=v_, sx=sx: e.bn_stats(out=sx[:, 0:6], in_=v_[:, 0:512]), [vk], [f'vst{i2}'])
                    S.op('dve', lambda e, v_=v_, sx=sx: e.bn_stats(out=sx[:, 6:12], in_=v_[:, 512:1024]), [vk], [f'vst{i2}'])
                    S.op('dve', lambda e, sx=sx: e.bn_aggr(out=sx[:, 12:14], in_=sx[:, 0:12]), [f'vst{i2}'], [f'vmv{i2}'])
                    ts(sx[:, 14:15], sx[:, 13:14], LN_EPS, None, ALU.add, None, [f'vmv{i2}'], [f'vr{i2}'], eng='pool')
                    tt(sx[:, 14:15], sx[:, 14:15], small[:, 0:1], ALU.pow, [f'vr{i2}', 'mhalf'], [f'vr{i2}'], eng='pool')
                    ts(sx[:, 15:16], sx[:, 12:13], sx[:, 14:15], -1.0, ALU.mult, ALU.mult, [f'vmv{i2}', f'vr{i2}'], [f'vn{i2}'])
                    act(v_, v_, AF.Identity, [vk, f'vr{i2}', f'vn{i2}'], [vk], bias=sx[:, 15:16], scale=sx[:, 14:15])
                    tt(v_, v_, lg, ALU.mult, [vk, 'row_lg'], [vk])
                    tt(vnb[i2], v_, lb, ALU.add, [vk, 'row_lb'], [f'vnb{i2}'], eng='pool')

                def sgu_B(tc):
                    i2 = tc % 2
                    mx_ = mxs2[i2]
                    for g in range(8):
                        b = 2 + g // 4 + 2 * i2
                        mm(banks[b][:, (g % 4) * 128:(g % 4 + 1) * 128], vnb[i2][:, g * 128:(g + 1) * 128], spw[:, g, :], True, True,
                           [f'vnb{i2}', 'spw'], [f'ps{b}'])
                    for hh in range(2):
                        b = 2 + hh + 2 * i2
                        tt(mx_[:, hh * 512:(hh + 1) * 512], banks[b][:, :], spb[:, hh * 512:(hh + 1) * 512], ALU.add, [f'ps{b}', 'row_spb'],
                           [f'mxs{i2}'])
                    ug = uT[:, :, tc * 128:(tc + 1) * 128]
                    tt(ug, ug, mx_.rearrange("p (g q) -> p g q", q=128), ALU.mult, ['uT', f'mxs{i2}'], ['uT'])

                sgu_A(0)
                for tc in range(NT):
                    if tc + 1 < NT:
                        sgu_A(tc + 1)
                    sgu_B(tc)
                ws.release(wvv[0][1])
                ws.release(wvv[1][1])
                S.barrier()
                P4.release(mg)
                rows = dict(g2=rowsA[0], lng=rowsA[1], lnb=rowsA[2], s4p=rowsA[3], s3=P4.alloc([P, D], F32))
                adaln_row(1, 2, rows['g2'], 'row_g2')
                adaln_row(1, 3, rows['s3'], 'row_s3')
                adaln_row(1, 4, rows['s4p'], 'row_s4p', plus1=True)
                load_row(rows['lng'], 'ln_mix_g1', 'row_lng')
                load_row(rows['lnb'], 'ln_mix_b1', 'row_lnb')
                mixer_epilogue(1, [uT[:, kc, :] for kc in range(8)], ['uT'], sgu_out_w, 'w_out1_', rows)
                S.barrier()
                P4.release(m4)

            B2 = Arena(arena_t, slot4_base, 2 * 8192)

            layer0_mixer()
            if stage >= 2:
                moe(0, last=False)
            if stage >= 3:
                B2.release(0)
                layer1_mixer()
            if stage >= 4:
                moe(1, last=True)
            if stage < 4:
                S.barrier()
                for tc in range(NT):
                    dma(y_d[tc * 128:(tc + 1) * 128, :], xres[:, tc, :], [('xres', tc)], ['y_d'])
            S.barrier()


        S1 = Sched()
        ws1 = WStream(S1, slots, plan=None)
        tops = (A.top, P4.top, R1.top)
        emit_all(S1, ws1)
        plan = ws1.rec
        A.top, P4.top, R1.top = tops
        S2 = Sched()
        ws2 = WStream(S2, slots, plan=plan)
        emit_all(S2, ws2)
        print(f"[kernel] ops={S2.nops} pieces={len(plan)} sems={S2.cnt}", flush=True)
        S2.replay(nc, es)
    return nc


def _prep_shared(inp):
    f = lambda k: np.ascontiguousarray(np.asarray(inp[k], dtype=np.float32))
    sh = {}
    sh["mod_w"] = f("mod_w")
    w = f("ab_in_w")[0]
    swp = np.arange(128) ^ 1
    q = w[:, 0:512].reshape(D, 4, 128)
    k = w[:, 512:768].reshape(D, 2, 128)
    sh["w_in"] = np.ascontiguousarray(np.concatenate([w, q[:, :, swp].reshape(D, 512), k[:, :, swp].reshape(D, 256)], axis=1))
    sh["ab_out_w"] = f("ab_out_w")[0]
    bd = np.zeros((2, 2, 4, 128, 128), np.float32)
    wa, wx = f("lru_wa")[0], f("lru_wx")[0]
    for dr in range(2):
        for gi, wsrc in enumerate((wa, wx)):
            for blk in range(8):
                c, o = blk // 2, (blk % 2) * 64
                bd[dr, gi, c, o:o + 64, o:o + 64] = wsrc[dr, blk]
    sh["bd"] = bd.reshape(16, 128, 128)
    sh["sgu_in_w"] = f("sgu_in_w")[0]
    sh["spwT"] = np.ascontiguousarray(f("sgu_spatial_w")[0].transpose(2, 0, 1).reshape(128, 1024))
    sh["sgu_out_w"] = f("sgu_out_w")[0]
    sh["router_w"] = f("router_w")
    sh["moe_w1"] = f("moe_w1")
    sh["moe_w3"] = f("moe_w3")
    sh["moe_w2"] = f("moe_w2")
    return sh


def _rope_tables():
    t = np.arange(T)
    row = (t // 64).astype(np.float32)
    col = (t % 64).astype(np.float32)
    nf = 32
    freqs = (np.float32(10000.0) ** (-np.arange(nf, dtype=np.float32) / np.float32(nf))).astype(np.float32)
    ang = np.concatenate([row[:, None] * freqs, col[:, None] * freqs], axis=-1).astype(np.float32)
    cos, sin = np.cos(ang), np.sin(ang)
    dd = np.arange(128)
    C = cos[:, dd // 2].T
    Ss = sin[:, dd // 2].T * np.where(dd % 2 == 0, -1.0, 1.0)[:, None]
    return np.ascontiguousarray(np.concatenate([C, Ss], axis=1).astype(np.float32))


def _masks(is_sample):
    m = np.zeros((128, 16, 2, 128), np.float32)
    kk = np.arange(128)[:, None]
    qq = np.arange(128)[None, :]
    for n in range(16):
        if is_sample:
            m[:, n, 0, :] = np.where(kk >= qq, 0.0, NEGM) if n >= 1 else NEGM
            m[:, n, 1, :] = np.where(kk <= qq, 0.0, NEGM) if n <= 14 else NEGM
        else:
            m[:, n, 0, :] = 0.0 if n % 2 == 1 else NEGM
            m[:, n, 1, :] = 0.0 if n % 2 == 0 else NEGM
    return np.ascontiguousarray(m.reshape(128, -1))


def _prep_core(inp, core):
    f = lambda k: np.asarray(inp[k], dtype=np.float32)
    is_s = core < 4
    m = {}
    if is_s:
        m["x_in"] = np.ascontiguousarray(f("x_sample")[core])
        cv = f("c")[core]
    else:
        m["x_in"] = np.ascontiguousarray(f("x_prompt")[8 * (core - 4):8 * (core - 3)].reshape(T, D))
        cv = f("c_ctx")
    V = np.zeros((NV, 128), np.float32)

    def put(name, arr):
        b, n = VR[name]
        V[b:b + n] = np.asarray(arr, np.float32).reshape(n, 128)
    put('c', cv)
    mb = f("mod_b")
    for l in range(2):
        put(f'modb{l}_0', mb[l, 0:1024])
        put(f'modb{l}_1', mb[l, 1024:2048])
    put('convw', f("rnn_conv_w")[0])
    put('convb', f("rnn_conv_b")[0])
    put('ba', f("lru_ba")[0])
    put('bx', f("lru_bx")[0])
    put('lam', f("lru_lambda")[0])
    put('h0', f("state_rglru")[core, 0] if is_s else np.zeros((2, 512), np.float32))
    put('sgub_u', f("sgu_in_b")[0, 0:1024])
    fl = np.zeros((4, 128), np.float32)
    fl[0] = 1.0 if is_s else 0.0
    fl[1] = 0.0 if is_s else NEGM
    fl[2] = 1.0 if is_s else 0.0
    fl[3] = 0.0 if is_s else 1.0
    put('flags', fl)
    put('sink', np.repeat(f("attn_sink")[0][:, None], 128, axis=1))
    m["vecs"] = V
    Rw = np.zeros((NR, D), np.float32)
    for l in range(2):
        for i in range(2, 6):
            Rw[RR[f'modb{l}_{i}']] = mb[l, i * 1024:(i + 1) * 1024]
        for nme in ('ln_mix_g', 'ln_mix_b', 'ln_ffn_g', 'ln_ffn_b'):
            Rw[RR[f'{nme}{l}']] = f(nme)[l]
    Rw[RR['sgub_v']] = f("sgu_in_b")[0, 1024:2048]
    Rw[RR['sgu_ln_g']] = f("sgu_ln_g")[0]
    Rw[RR['sgu_ln_b']] = f("sgu_ln_b")[0]
    Rw[RR['spb']] = f("sgu_spatial_b")[0].reshape(-1)
    m["rowsrc"] = Rw
    if is_s:
        ck = f("cache_k")[core, 0]
        cvv = f("cache_v")[core, 0]
        m["kctxT"] = np.ascontiguousarray(ck.transpose(2, 1, 0).reshape(128, 512))
        m["vctx"] = np.ascontiguousarray(cvv.reshape(2, 128, 256).transpose(1, 0, 2).reshape(128, 512))
        m["rope"] = _rope_tables()
    else:
        m["kctxT"] = np.zeros((128, 512), np.float32)
        m["vctx"] = np.zeros((128, 512), np.float32)
        m["rope"] = np.ascontiguousarray(np.concatenate([np.ones((128, T), np.float32), np.zeros((128, T), np.float32)], axis=1))
    m["maskLR"] = _masks(is_s)
    return m


_CACHE = {}


def kernel(**inputs):
    stage = int(os.environ.get("MK_STAGE", "99"))
    cut = int(os.environ.get("MK_CUT", "99"))
    cores = [int(c) for c in os.environ.get("MK_CORES", "0,1,2,3,4,5,6,7").split(",")]
    if (stage, cut) not in _CACHE:
        _CACHE[(stage, cut)] = build_program(stage, cut)
    nc = _CACHE[(stage, cut)]
    sh = _prep_shared(inputs)
    if stage < 2:
        for k in ("moe_w1", "moe_w3", "moe_w2"):
            sh.pop(k)
    in_maps = []
    for core in cores:
        m = _prep_core(inputs, core)
        m.update(sh)
        in_maps.append(m)
    res = run_bass_kernel_spmd(nc, in_maps, core_ids=list(range(len(cores))))
    R = res.results
    if len(cores) < 8:
        R = {c: R[i] for i, c in enumerate(cores)}
        for c in range(8):
            R.setdefault(c, R[cores[0]] if c < 4 else R[cores[-1]])
    y_sample = np.stack([R[c]["y"] for c in range(4)], 0).astype(np.float32)
    y_prompt = np.concatenate([R[c]["y"].reshape(8, 256, D) for c in range(4, 8)], 0).astype(np.float32)
    kv = np.concatenate([R[c]["kv_out"].reshape(8, 256, 512) for c in range(4, 8)], 0)
    new_k = np.ascontiguousarray(kv[:, :, 0:256].reshape(32, 1, 256, 2, 128)).astype(np.float32)
    new_v = np.ascontiguousarray(kv[:, :, 256:512].reshape(32, 1, 256, 2, 128)).astype(np.float32)
    st = np.concatenate([R[c]["st_out"].reshape(8, 1, 2, 512) for c in range(4, 8)], 0).astype(np.float32)
    return (y_prompt, y_sample, new_k, new_v, st)
```
